# Optimizing a Trainium2 kernel written in Bass

```python
import jax, jax.numpy as jnp
from jax import lax
import numpy as np

D_MODEL = 1024
BATCH = 4
SEQ = 4096
DEPTH = 4
DEC_BATCH = 16
DEC_SEQ = 2048
PAST_LEN = 128

N_MIXERS = 3
RMS_EPS = 1e-6
D_FF = ((8 * D_MODEL // 3 + 255) // 256) * 256

GRID_W = 64
NA_HEADS = 16
NA_HEAD_DIM = D_MODEL // NA_HEADS
NA_KH_MAX = 8
NA_KW = 16
NA_QBW = 16
NA_KBW = NA_QBW + NA_KW
NA_NCB = GRID_W // NA_QBW

LRU_WIDTH = ((4 * D_MODEL // 3 + 127) // 128) * 128
LRU_BLOCKS = 16
LRU_BLOCK = LRU_WIDTH // LRU_BLOCKS
CONV_W = 4
CONV_PAD = (2, 1)
LRU_C = 8.0

MLA_HEADS = 16
MLA_Q_RANK = 384
MLA_KV_RANK = 256
MLA_NOPE = 64
MLA_ROPE = 32
MLA_V = 64
ROPE_THETA = 10000.0
Q_BLOCK = 128

kernel_name = "hybrid_na_rglru_mla_encoder"


def rms_norm(x, g):
    xf = x.astype(jnp.float32)
    y = xf * lax.rsqrt(jnp.mean(xf * xf, axis=-1, keepdims=True) + RMS_EPS)
    return (y * g.astype(jnp.float32)).astype(x.dtype)


def swiglu(x, w_gate, w_up, w_down):
    return (jax.nn.silu(x @ w_gate) * (x @ w_up)) @ w_down


def _na_tables():
    qc = np.arange(GRID_W)
    win_start = np.clip(qc - NA_KW // 2, 0, GRID_W - NA_KW)
    band_start = np.clip(np.arange(NA_NCB) * NA_QBW - NA_KW // 2, 0, GRID_W - NA_KBW)
    key_cols = band_start[:, None] + np.arange(NA_KBW)
    qcols = qc.reshape(NA_NCB, NA_QBW)
    kc = key_cols[:, None, :]
    ws = win_start[qcols][:, :, None]
    valid = (kc >= ws) & (kc < ws + NA_KW)
    dc_idx = np.clip(kc - qcols[:, :, None] + NA_KW - 1, 0, 2 * NA_KW - 2)
    return key_cols, valid, dc_idx


def neighbourhood_attention(x, w_qkv, rpb, w_o):
    B, T, _ = x.shape
    rows = T // GRID_W
    kh = min(NA_KH_MAX, rows)
    key_cols, valid, dc_idx = _na_tables()
    qkv = (x @ w_qkv).reshape(B, rows, GRID_W, 3, NA_HEADS, NA_HEAD_DIM)
    q = qkv[:, :, :, 0].reshape(B, rows, NA_NCB, NA_QBW, NA_HEADS, NA_HEAD_DIM)
    k = jnp.take(qkv[:, :, :, 1], jnp.asarray(key_cols), axis=2)
    v = jnp.take(qkv[:, :, :, 2], jnp.asarray(key_cols), axis=2)
    valid_j = jnp.asarray(valid)[:, :, None, :]
    dc_j = jnp.asarray(dc_idx)
    scale = NA_HEAD_DIM ** -0.5

    def row_step(r):
        rs = jnp.clip(r - kh // 2, 0, rows - kh)
        kr = lax.dynamic_slice_in_dim(k, rs, kh, axis=1)
        vr = lax.dynamic_slice_in_dim(v, rs, kh, axis=1)
        qr = lax.dynamic_index_in_dim(q, r, axis=1, keepdims=False)
        s = jnp.einsum('bjqhd,bkjchd->bhjqkc', qr, kr).astype(jnp.float32) * scale
        dr_idx = rs + jnp.arange(kh) - r + NA_KH_MAX - 1
        bias = jnp.transpose(rpb[:, dr_idx][:, :, dc_j], (0, 2, 3, 1, 4))
        s = jnp.where(valid_j, s + bias.astype(jnp.float32), -jnp.inf)
        p = jax.nn.softmax(s.reshape(B, NA_HEADS, NA_NCB, NA_QBW, kh * NA_KBW), axis=-1)
        p = p.reshape(s.shape).astype(x.dtype)
        return jnp.einsum('bhjqkc,bkjchd->bjqhd', p, vr)

    o = lax.map(row_step, jnp.arange(rows))
    o = jnp.moveaxis(o, 0, 1).reshape(B, T, D_MODEL)
    return o @ w_o


def _lin_comb(e1, e2):
    a1, b1 = e1
    a2, b2 = e2
    return a1 * a2, a2 * b1 + b2


def rglru_direction(xc, w_a, b_a, w_x, b_x, lam, reverse):
    B, T, C = xc.shape
    xb = xc.reshape(B, T, LRU_BLOCKS, LRU_BLOCK)
    r = jax.nn.sigmoid((jnp.einsum('btnc,ncd->btnd', xb, w_a).reshape(B, T, C) + b_a).astype(jnp.float32))
    i = jax.nn.sigmoid((jnp.einsum('btnc,ncd->btnd', xb, w_x).reshape(B, T, C) + b_x).astype(jnp.float32))
    log_a = -LRU_C * r * jax.nn.softplus(-lam.astype(jnp.float32))
    a = jnp.exp(log_a)
    u = jnp.sqrt(-jnp.expm1(2.0 * log_a)) * (i * xc.astype(jnp.float32))
    _, h = lax.associative_scan(_lin_comb, (a, u), axis=1, reverse=reverse)
    return h


def recurrent_block(x, w_in, conv_w, conv_b, w_a, b_a, w_x, b_x, lam, w_out):
    gate, xr = jnp.split(x @ w_in, 2, axis=-1)
    xc = lax.conv_general_dilated(xr, conv_w[:, None, :], window_strides=(1,), padding=[CONV_PAD],
                                  dimension_numbers=('NWC', 'WIO', 'NWC'),
                                  feature_group_count=LRU_WIDTH) + conv_b
    h = (rglru_direction(xc, w_a[0], b_a[0], w_x[0], b_x[0], lam[0], False)
         + rglru_direction(xc, w_a[1], b_a[1], w_x[1], b_x[1], lam[1], True))
    y = jax.nn.gelu(gate.astype(jnp.float32)) * h
    return y.astype(x.dtype) @ w_out


def apply_rope(x, cos, sin):
    x1, x2 = jnp.split(x.astype(jnp.float32), 2, axis=-1)
    return jnp.concatenate([x1 * cos - x2 * sin, x1 * sin + x2 * cos], axis=-1).astype(x.dtype)


def mla(x, w_dq, g_q, w_uq, w_dkv, g_kv, w_ukv, w_o):
    B, T, _ = x.shape
    pos = jnp.arange(T, dtype=jnp.float32)
    inv = ROPE_THETA ** (-jnp.arange(0, MLA_ROPE, 2, dtype=jnp.float32) / MLA_ROPE)
    ang = pos[:, None] * inv[None, :]
    cos, sin = jnp.cos(ang), jnp.sin(ang)
    c_q = rms_norm(x @ w_dq, g_q)
    q = (c_q @ w_uq).reshape(B, T, MLA_HEADS, MLA_NOPE + MLA_ROPE)
    q_nope = q[..., :MLA_NOPE]
    q_rope = apply_rope(q[..., MLA_NOPE:], cos[:, None, :], sin[:, None, :])
    kv_a = x @ w_dkv
    c_kv = rms_norm(kv_a[..., :MLA_KV_RANK], g_kv)
    k_rope = apply_rope(kv_a[..., MLA_KV_RANK:], cos, sin)
    kv = (c_kv @ w_ukv).reshape(B, T, MLA_HEADS, MLA_NOPE + MLA_V)
    k_nope, v = kv[..., :MLA_NOPE], kv[..., MLA_NOPE:]
    scale = (MLA_NOPE + MLA_ROPE) ** -0.5
    nqb = T // Q_BLOCK
    qn_b = jnp.moveaxis(q_nope.reshape(B, nqb, Q_BLOCK, MLA_HEADS, MLA_NOPE), 1, 0)
    qr_b = jnp.moveaxis(q_rope.reshape(B, nqb, Q_BLOCK, MLA_HEADS, MLA_ROPE), 1, 0)

    def q_block(args):
        qn, qr = args
        s = (jnp.einsum('bqhd,bkhd->bhqk', qn, k_nope)
             + jnp.einsum('bqhd,bkd->bhqk', qr, k_rope)).astype(jnp.float32) * scale
        p = jax.nn.softmax(s, axis=-1).astype(x.dtype)
        return jnp.einsum('bhqk,bkhd->bqhd', p, v)

    o = lax.map(q_block, (qn_b, qr_b))
    o = jnp.moveaxis(o, 0, 1).reshape(B, T, MLA_HEADS * MLA_V)
    return o @ w_o


def setup_inputs(seed: int = 0) -> dict:
    key = jax.random.key(seed)
    ks = iter(jax.random.split(key, 40))
    n_a = len(range(0, DEPTH, N_MIXERS))
    n_b = len(range(1, DEPTH, N_MIXERS))
    n_c = len(range(2, DEPTH, N_MIXERS))
    C = LRU_WIDTH

    def dense(shape, fan_in):
        return jax.random.normal(next(ks), shape, jnp.float32) * fan_in ** -0.5

    def gain(shape):
        return 1.0 + 0.05 * jax.random.normal(next(ks), shape, jnp.float32)

    def small(shape, s):
        return s * jax.random.normal(next(ks), shape, jnp.float32)

    x_prompt = jax.random.normal(next(ks), (BATCH, SEQ, D_MODEL), jnp.float32)
    x_sample = jax.random.normal(next(ks), (DEC_BATCH, DEC_SEQ, D_MODEL), jnp.float32)
    a_c = jax.random.uniform(next(ks), (n_b, 2, C), jnp.float32, 0.9, 0.999)
    s = a_c ** (1.0 / LRU_C)
    lru_lam = jnp.log(s) - jnp.log1p(-s)
    return {
        "x_prompt": x_prompt,
        "x_sample": x_sample,
        "norm_mix": gain((DEPTH, D_MODEL)),
        "norm_ffn": gain((DEPTH, D_MODEL)),
        "norm_final": gain((D_MODEL,)),
        "na_w_qkv": dense((n_a, D_MODEL, 3 * D_MODEL), D_MODEL),
        "na_rpb": small((n_a, NA_HEADS, 2 * NA_KH_MAX - 1, 2 * NA_KW - 1), 0.2),
        "na_w_o": dense((n_a, D_MODEL, D_MODEL), D_MODEL),
        "lru_w_in": dense((n_b, D_MODEL, 2 * C), D_MODEL),
        "lru_conv_w": dense((n_b, CONV_W, C), CONV_W),
        "lru_conv_b": small((n_b, C), 0.02),
        "lru_w_a": dense((n_b, 2, LRU_BLOCKS, LRU_BLOCK, LRU_BLOCK), LRU_BLOCK),
        "lru_b_a": small((n_b, 2, C), 0.1),
        "lru_w_x": dense((n_b, 2, LRU_BLOCKS, LRU_BLOCK, LRU_BLOCK), LRU_BLOCK),
        "lru_b_x": small((n_b, 2, C), 0.1),
        "lru_lam": lru_lam,
        "lru_w_out": dense((n_b, C, D_MODEL), C),
        "mla_w_dq": dense((n_c, D_MODEL, MLA_Q_RANK), D_MODEL),
        "mla_g_q": gain((n_c, MLA_Q_RANK)),
        "mla_w_uq": dense((n_c, MLA_Q_RANK, MLA_HEADS * (MLA_NOPE + MLA_ROPE)), MLA_Q_RANK),
        "mla_w_dkv": dense((n_c, D_MODEL, MLA_KV_RANK + MLA_ROPE), D_MODEL),
        "mla_g_kv": gain((n_c, MLA_KV_RANK)),
        "mla_w_ukv": dense((n_c, MLA_KV_RANK, MLA_HEADS * (MLA_NOPE + MLA_V)), MLA_KV_RANK),
        "mla_w_o": dense((n_c, MLA_HEADS * MLA_V, D_MODEL), MLA_HEADS * MLA_V),
        "ffn_w_gate": dense((DEPTH, D_MODEL, D_FF), D_MODEL),
        "ffn_w_up": dense((DEPTH, D_MODEL, D_FF), D_MODEL),
        "ffn_w_down": dense((DEPTH, D_FF, D_MODEL), D_FF),
    }


def reference(x_prompt, x_sample, norm_mix, norm_ffn, norm_final,
              na_w_qkv, na_rpb, na_w_o,
              lru_w_in, lru_conv_w, lru_conv_b, lru_w_a, lru_b_a, lru_w_x, lru_b_x, lru_lam, lru_w_out,
              mla_w_dq, mla_g_q, mla_w_uq, mla_w_dkv, mla_g_kv, mla_w_ukv, mla_w_o,
              ffn_w_gate, ffn_w_up, ffn_w_down):
    def run(x):
        for i in range(DEPTH):
            j = i // N_MIXERS
            m = i % N_MIXERS
            h = rms_norm(x, norm_mix[i])
            if m == 0:
                y = neighbourhood_attention(h, na_w_qkv[j], na_rpb[j], na_w_o[j])
            elif m == 1:
                y = recurrent_block(h, lru_w_in[j], lru_conv_w[j], lru_conv_b[j], lru_w_a[j], lru_b_a[j],
                                    lru_w_x[j], lru_b_x[j], lru_lam[j], lru_w_out[j])
            else:
                y = mla(h, mla_w_dq[j], mla_g_q[j], mla_w_uq[j], mla_w_dkv[j], mla_g_kv[j],
                        mla_w_ukv[j], mla_w_o[j])
            x = x + y
            x = x + swiglu(rms_norm(x, norm_ffn[i]), ffn_w_gate[i], ffn_w_up[i], ffn_w_down[i])
        return rms_norm(x, norm_final)

    y_prompt = run(x_prompt)
    y_sample = run(x_sample)
    return (y_prompt, y_sample)
```

```python
import numpy as np
import concourse.bass as bass
import concourse.mybir as mybir
from concourse.bass_utils import run_bass_kernel_spmd

F32 = mybir.dt.float32
BF16 = mybir.dt.bfloat16
AF = mybir.ActivationFunctionType
ALU = mybir.AluOpType

D = 1024
KC = 8
DFF = 2816
FC = 22
GW = 64
NH = 16
LRU_C = 1408
LC = 11
NEG = -30000.0
EPS = 1e-6
GELU_C1 = 0.7978845608028654
GELU_C2 = 0.7978845608028654 * 0.044715


class Cfg:
    def __init__(self, seg_rows=32, n_cores=8, layers=(0, 1, 2, 3), do_ffn=True):
        self.R = seg_rows
        self.SEG = seg_rows * GW
        self.NT = 3 * self.SEG
        self.n_cores = n_cores
        self.layers = tuple(layers)
        self.do_ffn = do_ffn


class Buf:
    __slots__ = ("name", "w", "readers")

    def __init__(self, name=""):
        self.name = name
        self.w = None
        self.readers = []


class Op:
    __slots__ = ("eng", "emit", "waits", "signal", "seq", "is_dma", "sem", "semval")


COMPUTE = ("pe", "act", "dve", "pool")
QUEUES = ("pe", "act", "dve", "pool", "sp")


class Tracker:
    def __init__(self, nc, n_dma_sems=20):
        self.nc = nc
        self.ops = {q: [] for q in QUEUES}
        self.seq = {q: 0 for q in COMPUTE}
        self.waited = {q: {c: 0 for c in COMPUTE} for q in QUEUES}
        self.waited_dma = {q: set() for q in QUEUES}
        self.dma_since_barrier = []
        self.n_dma_sems = n_dma_sems
        self.dma_count = {"sp": 0, "pool": 0}
        self.dma_last_on_sem = {"sp": [None] * n_dma_sems, "pool": [None] * n_dma_sems}
        self.dma_sem_total = {"sp": [0] * n_dma_sems, "pool": [0] * n_dma_sems}

    def _dep(self, op, d, waits):
        q = op.eng
        if d is None:
            return
        if d.is_dma:
            if d in self.waited_dma[q]:
                return
            self.waited_dma[q].add(d)
            waits.append(d)
        else:
            if d.eng == q and q == "pe" and not op.is_dma:
                return
            if self.waited[q][d.eng] >= d.seq:
                return
            self.waited[q][d.eng] = d.seq
            d.signal = True
            waits.append(d)

    def add(self, eng, emit, r=(), w=(), dma=False):
        op = Op()
        op.eng = eng
        op.emit = emit
        op.is_dma = dma
        op.signal = False
        op.sem = None
        op.semval = 0
        waits = []
        for b in r:
            self._dep(op, b.w, waits)
        for b in w:
            self._dep(op, b.w, waits)
            for rd in b.readers:
                self._dep(op, rd, waits)
        if dma:
            j = self.dma_count[eng]
            self.dma_count[eng] += 1
            s = j % self.n_dma_sems
            prev = self.dma_last_on_sem[eng][s]
            if prev is not None:
                self._dep(op, prev, waits)
            self.dma_last_on_sem[eng][s] = op
            self.dma_sem_total[eng][s] += 16
            op.sem = (eng, s)
            op.semval = self.dma_sem_total[eng][s]
            op.seq = -1
            self.dma_since_barrier.append(op)
        else:
            self.seq[eng] += 1
            op.seq = self.seq[eng]
        op.waits = waits
        for b in r:
            if not dma:
                b.readers = [x for x in b.readers if x.is_dma or x.eng != eng]
            b.readers.append(op)
        for b in w:
            b.w = op
            b.readers = []
        self.ops[eng].append(op)
        return op

    def barrier(self):
        last = {}
        for c in COMPUTE:
            for op in reversed(self.ops[c]):
                if not op.is_dma and op.emit is not None:
                    last[c] = op
                    break
        dmas = list(self.dma_since_barrier)
        self.dma_since_barrier = []
        for q in QUEUES:
            op = Op()
            op.eng = q
            op.emit = None
            op.is_dma = False
            op.signal = False
            op.sem = None
            op.semval = 0
            op.seq = self.seq[q] if q in COMPUTE else 0
            waits = []
            fake = Op()
            fake.eng = q
            fake.is_dma = True
            for c, l in last.items():
                if c == q:
                    continue
                self._dep(fake, l, waits)
            for d in dmas:
                self._dep(fake, d, waits)
            op.waits = waits
            self.ops[q].append(op)

    def finalize(self, sems_compute, sems_dma):
        for c in COMPUTE:
            cnt = 0
            for op in self.ops[c]:
                if op.is_dma or op.emit is None:
                    continue
                if op.signal:
                    cnt += 1
                    op.semval = cnt
                    op.sem = sems_compute[c]
        for q in QUEUES:
            for op in self.ops[q]:
                if op.is_dma:
                    op.sem = sems_dma[op.sem[0]][op.sem[1]]

    def emit_queue(self, q, e):
        for op in self.ops[q]:
            for d in op.waits:
                e.wait_ge(d.sem, d.semval)
            if op.emit is None:
                continue
            ins = op.emit(e)
            if op.is_dma:
                ins.then_inc(op.sem, 16)
            elif op.signal:
                ins.then_inc(op.sem, 1)


class Arena:
    def __init__(self, nc, base, limit):
        self.nc = nc
        self.base = base
        self.limit = limit
        self.off = base
        self.uid = 0

    def reset(self):
        self.off = self.base

    def alloc(self, shape, dtype, name="t"):
        esz = 4 if dtype == F32 else 2
        n = 1
        for s in shape[1:]:
            n *= s
        nbytes = (n * esz + 63) // 64 * 64
        off = self.off
        assert off + nbytes <= self.limit, f"SBUF arena overflow: {name} {shape} at {off} (+{nbytes}) limit {self.limit}"
        self.off += nbytes
        self.uid += 1
        return self.nc.alloc_sbuf_tensor_at(f"{name}_{self.uid}", list(shape), dtype, offset=off)


class Ring:
    def __init__(self, arena, n, shape, dtype, name):
        self.tiles = [arena.alloc(shape, dtype, name) for _ in range(n)]
        self.bufs = [Buf(f"{name}{i}") for i in range(n)]
        self.i = 0

    def next(self):
        k = self.i % len(self.tiles)
        self.i += 1
        return self.tiles[k], self.bufs[k]


class PV:
    def __init__(self):
        self.n = 0
        self.off = {}

    def add(self, name, ncol):
        self.off[name] = self.n
        self.n += ncol


def pv_layout():
    pv = PV()
    for i in range(4):
        pv.add(f"nmix{i}", KC)
        pv.add(f"nffn{i}", KC)
    pv.add("nfin", KC)
    pv.add("convw", LC * 4)
    pv.add("convb", LC)
    for d in range(2):
        pv.add(f"ba{d}", LC)
        pv.add(f"bx{d}", LC)
        pv.add(f"lam{d}", LC)
    pv.add("gq", 3)
    pv.add("gkv", 2)
    pv.add("flag", 1)
    pv.add("xbias", 1)
    pv.add("nab", 48)
    pv.add("zero", 1)
    pv.add("eps", 1)
    pv.add("one", 1)
    return pv


class Prog:
    def __init__(self, cfg):
        self.cfg = cfg
        self.nc = bass.Bass("TRN2", target_bir_lowering=False)
        self.T = Tracker(self.nc)
        self.pv = pv_layout()
        nc = self.nc
        NT = cfg.NT
        di = lambda name, shape, dt=F32: nc.dram_tensor(name, list(shape), dt, kind="ExternalInput")
        self.x_tok = di("x_tok", [NT, D])
        self.pvec_d = di("pvec", [128, self.pv.n])
        self.ident_d = di("ident", [128, 128])
        self.ffn_wg = di("ffn_wg", [4, D, DFF])
        self.ffn_wu = di("ffn_wu", [4, D, DFF])
        self.ffn_wd = di("ffn_wd", [4, DFF, D])
        self.na_wqkv = di("na_wqkv", [2, D, 3 * D])
        self.na_wo = di("na_wo", [2, D, D])
        self.na_btab = di("na_btab", [2, NH, 128, 1536])
        self.lru_win = di("lru_win", [D, 2 * LRU_C])
        self.lru_band = di("lru_band", [4, 128, LC * 3 * 128])
        self.lru_wout = di("lru_wout", [LRU_C, D])
        self.mla_wdq = di("mla_wdq", [D, 384])
        self.mla_wdkv = di("mla_wdkv", [D, 256])
        self.mla_wkr = di("mla_wkr", [2, D, 96])
        self.mla_wuq = di("mla_wuq", [2, 384, 1536])
        self.mla_wuk = di("mla_wuk", [256, 1024])
        self.mla_wuv = di("mla_wuv", [256, 1024])
        self.mla_wo = di("mla_wo", [D, D])
        self.rope_d = di("rope", [2, 32, NT])
        self.y_tok = nc.dram_tensor("y_tok", [NT, D], F32, kind="ExternalOutput")
        self.xT = nc.dram_tensor("xT_s", [KC, 128, NT], F32)
        self.xT2 = nc.dram_tensor("xT2_s", [KC, 128, NT], F32)
        self.gg_s = nc.dram_tensor("gg_s", [LC, 128, NT], F32)
        self.xc_s = nc.dram_tensor("xc_s", [LC, 128, NT], F32)
        self.hf_s = nc.dram_tensor("hf_s", [LC, 128, NT], F32)
        self.oT_s = nc.dram_tensor("oT_s", [KC, 128, NT], BF16)
        nblk = NT // 128
        self.xbuf = [Buf(f"x{i}") for i in range(nblk)]
        self.xbuf2 = [Buf(f"xb{i}") for i in range(nblk)]
        self.ggbuf = [Buf(f"gg{i}") for i in range(nblk)]
        self.xcbuf = [Buf(f"xc{i}") for i in range(nblk)]
        self.hfbuf = [Buf(f"hf{i}") for i in range(nblk)]
        self.obuf = [Buf(f"o{i}") for i in range(nblk)]
        self.ybuf = [Buf(f"y{i}") for i in range(nblk)]
        self.persist = Arena(nc, 16512, 16512 + 6 * 1024)
        self.pvec = self.persist.alloc([128, self.pv.n], F32, "pvec")
        self.ident = self.persist.alloc([128, 128], F32, "ident")
        self.identb = self.persist.alloc([128, 128], BF16, "identb")
        self.onesb = self.persist.alloc([128, 128], BF16, "onesb")
        self.onesf = self.persist.alloc([128, 128], F32, "onesf")
        self.lruc = self.persist.alloc([128, 2, 4, LC], F32, "lruc")
        self.pbuf = Buf("persist")
        self.arena = Arena(nc, self.persist.limit, 229376)
        self.psum = nc.alloc_psum_tensor("psum", [128, 8, 512], F32)
        self.pb = [Buf(f"bank{i}") for i in range(8)]
        self.pbi = 0

    def xblocks(self, bufs, t0, n):
        return bufs[t0 // 128:(t0 + n + 127) // 128]

    def pcol(self, name, c=0, n=1):
        o = self.pv.off[name] + c
        return self.pvec[:, o:o + n]

    def bank(self):
        k = self.pbi % 8
        self.pbi += 1
        return k

    def setup(self):
        T = self.T
        T.add("sp", lambda e: e.dma_start(out=self.pvec[:, :], in_=self.pvec_d[:, :]), w=[self.pbuf], dma=True)
        T.add("sp", lambda e: e.dma_start(out=self.ident[:, :], in_=self.ident_d[:, :]), w=[self.pbuf], dma=True)
        T.add("dve", lambda e: e.tensor_copy(out=self.identb[:, :], in_=self.ident[:, :]), r=[self.pbuf], w=[self.pbuf])
        T.add("dve", lambda e: e.memset(self.onesb[:, :], 1.0), w=[self.pbuf])
        T.add("dve", lambda e: e.memset(self.onesf[:, :], 1.0), w=[self.pbuf])
        T.barrier()

    def load_w(self, dst, bufs, src2d, K, c0=None, c1=None):
        for k in range(K):
            if c0 is None:
                src = src2d[k * 128:(k + 1) * 128, :]
            else:
                src = src2d[k * 128:(k + 1) * 128, c0:c1]
            self.T.add("pool", (lambda e, d=dst[:, k, :], s=src: e.dma_start(out=d, in_=s)), w=[bufs[k]], dma=True)

    def rmsnorm(self, xt, xb, KCn, W, gname, h, hb, sq, sqb, rstd, rb, Dn, bk=None):
        T = self.T
        T.add("act", lambda e: e.activation(out=sq[:, 0:KCn, 0:W], in_=xt[:, 0:KCn, 0:W], func=AF.Square), r=[xb], w=[sqb])
        if bk is None:
            bk = self.bank()
        ps = self.psum[:, bk, 0:W]

        def mm(e):
            for c in range(KCn):
                ins = e.matmul(ps, lhsT=self.onesb[:, :], rhs=sq[:, c, 0:W], start=(c == 0), stop=(c == KCn - 1))
            return ins
        T.add("pe", mm, r=[sqb], w=[self.pb[bk]])
        T.add("act", lambda e: e.activation(out=rstd[:, 0:W], in_=ps, func=AF.Ln, scale=1.0 / Dn, bias=self.pcol("eps")),
              r=[self.pb[bk]], w=[rb])
        T.add("act", lambda e: e.activation(out=rstd[:, 0:W], in_=rstd[:, 0:W], func=AF.Exp, scale=-0.5), r=[rb], w=[rb])

        def sc(e):
            for c in range(KCn):
                ins = e.scalar_tensor_tensor(out=h[:, c, 0:W], in0=xt[:, c, 0:W], scalar=self.pcol(gname, c), in1=rstd[:, 0:W],
                                             op0=ALU.mult, op1=ALU.mult)
            return ins
        T.add("dve", sc, r=[xb, rb], w=[hb])

    def phase_in(self):
        T, cfg = self.T, self.cfg
        A = self.arena
        A.reset()
        xin = Ring(A, 3, [128, D], F32, "xin")
        xo = Ring(A, 3, [128, KC, 128], F32, "xo")
        for b in range(cfg.NT // 128):
            t0 = b * 128
            xi, xib = xin.next()
            T.add("sp", lambda e, xi=xi, t0=t0: e.dma_start(out=xi[:, :], in_=self.x_tok[t0:t0 + 128, :]), w=[xib], dma=True)
            b0, b1 = self.bank(), self.bank()
            xot, xob = xo.next()
            for half, bk in ((0, b0), (1, b1)):
                def tr(e, xi=xi, half=half, bk=bk):
                    for j in range(4):
                        c = half * 4 + j
                        ins = e.transpose(self.psum[:, bk, j * 128:(j + 1) * 128], xi[:, c * 128:(c + 1) * 128], self.ident[:, :])
                    return ins
                T.add("pe", tr, r=[xib], w=[self.pb[bk]])
                eng = "act" if half == 0 else "dve"
                if eng == "act":
                    T.add("act", lambda e, xot=xot, bk=bk, half=half: e.activation(
                        out=xot[:, half * 4:half * 4 + 4, :], in_=self.psum[:, bk, :], func=AF.Copy), r=[self.pb[bk]], w=[xob])
                else:
                    T.add("dve", lambda e, xot=xot, bk=bk, half=half: e.tensor_copy(
                        out=xot[:, half * 4:half * 4 + 4, :], in_=self.psum[:, bk, :]), r=[self.pb[bk]], w=[xob])
            T.add("pool", lambda e, xot=xot, t0=t0, dst=self.xT: e.dma_start(
                out=dst[:, :, t0:t0 + 128].rearrange("c p t -> p c t"), in_=xot[:, :, :]),
                r=[xob], w=[self.xbuf[b]], dma=True)
        T.barrier()

    def phase_out(self):
        T, cfg = self.T, self.cfg
        A = self.arena
        A.reset()
        TT = 512
        xr = Ring(A, 2, [128, KC, TT], F32, "fx")
        sqr = Ring(A, 2, [128, KC, TT], BF16, "fsq")
        rs = Ring(A, 2, [128, TT], F32, "frs")
        yo = Ring(A, 3, [128, D], F32, "fy")
        for it in range(cfg.NT // TT):
            t0 = it * TT
            xt, xb = xr.next()
            T.add("sp", lambda e, xt=xt, t0=t0, src=self.xT: e.dma_start(out=xt[:, :, :], in_=src[:, :, t0:t0 + TT].rearrange("c p t -> p c t")),
                  r=self.xblocks(self.xbuf, t0, TT), w=[xb], dma=True)
            sq, sqb = sqr.next()
            rstd, rb = rs.next()
            T.add("act", lambda e, sq=sq, xt=xt: e.activation(out=sq[:, :, :], in_=xt[:, :, :], func=AF.Square), r=[xb], w=[sqb])
            bk = self.bank()
            ps = self.psum[:, bk, 0:TT]

            def mm(e, sq=sq, ps=ps):
                for c in range(KC):
                    ins = e.matmul(ps, lhsT=self.onesb[:, :], rhs=sq[:, c, :], start=(c == 0), stop=(c == KC - 1))
                return ins
            T.add("pe", mm, r=[sqb], w=[self.pb[bk]])
            T.add("act", lambda e, rstd=rstd, ps=ps: e.activation(out=rstd[:, :], in_=ps, func=AF.Ln, scale=1.0 / D, bias=self.pcol("eps")),
                  r=[self.pb[bk]], w=[rb])
            T.add("act", lambda e, rstd=rstd: e.activation(out=rstd[:, :], in_=rstd[:, :], func=AF.Exp, scale=-0.5), r=[rb], w=[rb])

            def sc(e, xt=xt, rstd=rstd):
                for c in range(KC):
                    ins = e.scalar_tensor_tensor(out=xt[:, c, :], in0=xt[:, c, :], scalar=self.pcol("nfin", c), in1=rstd[:, :],
                                                 op0=ALU.mult, op1=ALU.mult)
                return ins
            T.add("dve", sc, r=[xb, rb], w=[xb])
            for s in range(TT // 128):
                yt, yb = yo.next()
                for half in range(2):
                    bk = self.bank()

                    def tr(e, xt=xt, s=s, half=half, bk=bk):
                        for j in range(4):
                            c = half * 4 + j
                            ins = e.transpose(self.psum[:, bk, j * 128:(j + 1) * 128], xt[:, c, s * 128:(s + 1) * 128], self.ident[:, :])
                        return ins
                    T.add("pe", tr, r=[xb], w=[self.pb[bk]])
                    if half == 0:
                        T.add("act", lambda e, yt=yt, bk=bk: e.activation(out=yt[:, 0:512], in_=self.psum[:, bk, :], func=AF.Copy),
                              r=[self.pb[bk]], w=[yb])
                    else:
                        T.add("dve", lambda e, yt=yt, bk=bk: e.tensor_copy(out=yt[:, 512:1024], in_=self.psum[:, bk, :]),
                              r=[self.pb[bk]], w=[yb])
                tt = t0 + s * 128
                T.add("pool", lambda e, yt=yt, tt=tt: e.dma_start(out=self.y_tok[tt:tt + 128, :], in_=yt[:, :]),
                      r=[yb], w=[self.ybuf[tt // 128]], dma=True)
        T.barrier()

    def phase_ffn(self, li):
        T, cfg = self.T, self.cfg
        A = self.arena
        A.reset()
        TT = 256
        wg = A.alloc([128, KC, DFF], BF16, "wg")
        wu = A.alloc([128, KC, DFF], BF16, "wu")
        wd = A.alloc([128, FC, D], BF16, "wd")
        wgb = [Buf() for _ in range(KC)]
        wub = [Buf() for _ in range(KC)]
        wdb = [Buf() for _ in range(FC)]
        self.load_w(wg, wgb, self.ffn_wg[li], KC)
        self.load_w(wu, wub, self.ffn_wu[li], KC)
        self.load_w(wd, wdb, self.ffn_wd[li], FC)
        xr = Ring(A, 2, [128, KC, TT], F32, "x")
        hr = Ring(A, 2, [128, KC, TT], BF16, "h")
        ar = Ring(A, 2, [128, FC, TT], BF16, "a")
        rs = Ring(A, 2, [128, TT], F32, "rs")
        sgr = Ring(A, 4, [128, TT], F32, "sg")
        gname = f"nffn{li}"
        for it in range(cfg.NT // TT):
            t0 = it * TT
            xt, xb = xr.next()
            T.add("sp", lambda e, xt=xt, t0=t0, src=self.xT: e.dma_start(out=xt[:, :, :], in_=src[:, :, t0:t0 + TT].rearrange("c p t -> p c t")),
                  r=self.xblocks(self.xbuf, t0, TT), w=[xb], dma=True)
            h, hb = hr.next()
            a, ab = ar.next()
            rstd, rb = rs.next()
            self.rmsnorm(xt, xb, KC, TT, gname, h, hb, a, ab, rstd, rb, D)
            for j in range(FC):
                bg, bu = self.bank(), self.bank()

                def mmg(e, j=j, bg=bg, h=h):
                    for k in range(KC):
                        ins = e.matmul(self.psum[:, bg, 0:TT], lhsT=wg[:, k, j * 128:(j + 1) * 128], rhs=h[:, k, :],
                                       start=(k == 0), stop=(k == KC - 1))
                    return ins

                def mmu(e, j=j, bu=bu, h=h):
                    for k in range(KC):
                        ins = e.matmul(self.psum[:, bu, 0:TT], lhsT=wu[:, k, j * 128:(j + 1) * 128], rhs=h[:, k, :],
                                       start=(k == 0), stop=(k == KC - 1))
                    return ins
                T.add("pe", mmg, r=[hb] + wgb, w=[self.pb[bg]])
                T.add("pe", mmu, r=[hb] + wub, w=[self.pb[bu]])
                sg, sgb = sgr.next()
                T.add("act", lambda e, sg=sg, bg=bg: e.activation(out=sg[:, :], in_=self.psum[:, bg, 0:TT], func=AF.Tanh, scale=0.5),
                      r=[self.pb[bg]], w=[sgb])
                T.add("dve", lambda e, sg=sg, bg=bg: e.scalar_tensor_tensor(out=sg[:, :], in0=sg[:, :], scalar=1.0, in1=self.psum[:, bg, 0:TT],
                                                                            op0=ALU.add, op1=ALU.mult), r=[sgb, self.pb[bg]], w=[sgb])
                T.add("dve", lambda e, sg=sg, bu=bu, a=a, j=j: e.scalar_tensor_tensor(out=a[:, j, :], in0=sg[:, :], scalar=0.5,
                                                                                      in1=self.psum[:, bu, 0:TT], op0=ALU.mult, op1=ALU.mult),
                      r=[sgb, self.pb[bu]], w=[ab])
            for m in range(KC):
                bk = self.bank()

                def mmd(e, m=m, bk=bk, a=a):
                    for j in range(FC):
                        ins = e.matmul(self.psum[:, bk, 0:TT], lhsT=wd[:, j, m * 128:(m + 1) * 128], rhs=a[:, j, :],
                                       start=(j == 0), stop=(j == FC - 1))
                    return ins
                T.add("pe", mmd, r=[ab] + wdb, w=[self.pb[bk]])
                T.add("dve", lambda e, xt=xt, m=m, bk=bk: e.tensor_tensor(out=xt[:, m, :], in0=xt[:, m, :], in1=self.psum[:, bk, 0:TT], op=ALU.add),
                      r=[self.pb[bk], xb], w=[xb])
            T.add("pool", lambda e, xt=xt, t0=t0, dst=self.xT: e.dma_start(out=dst[:, :, t0:t0 + TT].rearrange("c p t -> p c t"), in_=xt[:, :, :]),
                  r=[xb], w=self.xblocks(self.xbuf, t0, TT), dma=True)
        T.barrier()


    def na_plan(self, group, r):
        R = self.cfg.R
        rows = 2 * R if group == 0 else R
        if group == 0 and r in (R - 4, R - 2, R, R + 2):
            pidx = {R - 4: 0, R - 2: 1, R: 2, R + 2: 3}[r]
            return (R - 8 if r < R else R - 4), 6, "boundary", pidx
        if r < 4:
            return 0, 4, "border", None
        if r >= rows - 4:
            return rows - 8, 4, "border", None
        return r - 4, 5, "interior", None

    def phase_na(self, li):
        T, cfg = self.T, self.cfg
        A = self.arena
        R, SEG = cfg.R, cfg.SEG
        jn = li // 3
        TT = 256
        gname = f"nmix{li}"
        wqkv = self.na_wqkv[jn]
        segs = [(0, 0, 0, R, 0, R + 4), (0, 0, R, 2 * R, R - 4, 2 * R), (1, 2 * SEG, 0, R, 0, R)]
        for sg in segs:
            self.na_segment(li, *sg)
        self.xT, self.xT2 = self.xT2, self.xT
        self.xbuf, self.xbuf2 = self.xbuf2, self.xbuf

    def na_segment(self, li, group, gbase, q0, q1, kr0, kr1):
        T, cfg = self.T, self.cfg
        A = self.arena
        R, SEG = cfg.R, cfg.SEG
        jn = li // 3
        TT = 256
        gname = f"nmix{li}"
        wqkv = self.na_wqkv[jn]
        KLmax = (R + 4) * GW
        if True:
            A.reset()
            KL = (kr1 - kr0) * GW
            KT = A.alloc([128, KC, KLmax], BF16, "KT")
            V = A.alloc([128, KLmax // 128, D], BF16, "V")
            mark = A.off
            wk = A.alloc([128, KC, D], BF16, "wk")
            wv = A.alloc([128, KC, D], BF16, "wv")
            wkb = [Buf() for _ in range(KC)]
            wvb = [Buf() for _ in range(KC)]
            self.load_w(wk, wkb, wqkv, KC, D, 2 * D)
            self.load_w(wv, wvb, wqkv, KC, 2 * D, 3 * D)
            xr = Ring(A, 2, [128, KC, TT], F32, "x")
            hr = Ring(A, 2, [128, KC, TT], BF16, "h")
            sqr = Ring(A, 2, [128, KC, TT], BF16, "sq")
            rs = Ring(A, 2, [128, TT], F32, "rs")
            kvb = Buf("kv")
            kvb2 = Buf("kv2")
            for it in range(KL // TT):
                tl = it * TT
                t0 = gbase + kr0 * GW + tl
                xt, xb = xr.next()
                T.add("sp", lambda e, xt=xt, t0=t0, src=self.xT: e.dma_start(out=xt[:, :, :], in_=src[:, :, t0:t0 + TT].rearrange("c p t -> p c t")),
                      r=self.xblocks(self.xbuf, t0, TT), w=[xb], dma=True)
                h, hb = hr.next()
                sq, sqb = sqr.next()
                rstd, rb = rs.next()
                self.rmsnorm(xt, xb, KC, TT, gname, h, hb, sq, sqb, rstd, rb, D)
                for oc in range(KC):
                    bk = self.bank()

                    def mmk(e, oc=oc, bk=bk, h=h):
                        for k in range(KC):
                            ins = e.matmul(self.psum[:, bk, 0:TT], lhsT=wk[:, k, oc * 128:(oc + 1) * 128], rhs=h[:, k, :],
                                           start=(k == 0), stop=(k == KC - 1))
                        return ins
                    T.add("pe", mmk, r=[hb] + wkb, w=[self.pb[bk]])
                    if oc % 2 == 0:
                        T.add("act", lambda e, oc=oc, bk=bk, tl=tl: e.activation(out=KT[:, oc, tl:tl + TT], in_=self.psum[:, bk, 0:TT], func=AF.Copy),
                              r=[self.pb[bk]], w=[kvb])
                    else:
                        T.add("dve", lambda e, oc=oc, bk=bk, tl=tl: e.tensor_copy(out=KT[:, oc, tl:tl + TT], in_=self.psum[:, bk, 0:TT]),
                              r=[self.pb[bk]], w=[kvb2])
                for st in range(TT // 128):
                    for half in range(2):
                        bk = self.bank()

                        def mmv(e, st=st, half=half, bk=bk, h=h):
                            for k in range(KC):
                                ins = e.matmul(self.psum[:, bk, :], lhsT=h[:, k, st * 128:(st + 1) * 128], rhs=wv[:, k, half * 512:(half + 1) * 512],
                                               start=(k == 0), stop=(k == KC - 1))
                            return ins
                        T.add("pe", mmv, r=[hb] + wvb, w=[self.pb[bk]])
                        vt = (tl + st * 128) // 128
                        if half == 0:
                            T.add("act", lambda e, vt=vt, bk=bk: e.activation(out=V[:, vt, 0:512], in_=self.psum[:, bk, :], func=AF.Copy),
                                  r=[self.pb[bk]], w=[kvb])
                        else:
                            T.add("dve", lambda e, vt=vt, bk=bk: e.tensor_copy(out=V[:, vt, 512:1024], in_=self.psum[:, bk, :]),
                                  r=[self.pb[bk]], w=[kvb2])
            T.barrier()
            A.off = mark
            wq = A.alloc([128, KC, D], BF16, "wq")
            wo = A.alloc([128, KC, D], BF16, "wo")
            bt = A.alloc([128, NH, 1536], BF16, "bt")
            wqb = [Buf() for _ in range(KC)]
            wob = [Buf() for _ in range(KC)]
            btb = [Buf() for _ in range(NH)]
            self.load_w(wq, wqb, wqkv, KC, 0, D)
            for hh in range(NH):
                T.add("pool", lambda e, hh=hh: e.dma_start(out=bt[:, hh, :], in_=self.na_btab[jn][hh]), w=[btb[hh]], dma=True)
            self.load_w(wo, wob, self.na_wo[jn], KC)
            xr = Ring(A, 2, [128, KC, TT], F32, "x")
            hr = Ring(A, 1, [128, KC, TT], BF16, "h")
            rs = Ring(A, 1, [128, TT], F32, "rs")
            qr = Ring(A, 2, [128, KC, TT], BF16, "q")
            orr = Ring(A, 2, [128, KC, TT], BF16, "o")
            ptr = Ring(A, 2, [128, 2, 768], BF16, "pt")
            ptb2 = [Buf("ptb_o0"), Buf("ptb_o1")]
            psmb2 = [Buf("psm_o0"), Buf("psm_o1")]
            rdr = Ring(A, 2, [128, 256], F32, "rd")
            ucount = 0
            sumr = Ring(A, 2, [128, 2, 128], BF16, "psm")
            ntile = (q1 - q0) * GW // TT

            def prologue(it):
                r0 = q0 + it * (TT // GW)
                t0 = gbase + r0 * GW
                xt, xb = xr.next()
                T.add("sp", lambda e, xt=xt, t0=t0, src=self.xT: e.dma_start(out=xt[:, :, :], in_=src[:, :, t0:t0 + TT].rearrange("c p t -> p c t")),
                      r=self.xblocks(self.xbuf, t0, TT), w=[xb], dma=True)
                h, hb = hr.next()
                rstd, rb = rs.next()
                self.rmsnorm(xt, xb, KC, TT, gname, h, hb, h, hb, rstd, rb, D, bk=6)
                qt, qb = qr.next()
                for oc in range(KC):
                    bk = 6 + (oc % 2)

                    def mmq(e, oc=oc, bk=bk, h=h):
                        for k in range(KC):
                            ins = e.matmul(self.psum[:, bk, 0:TT], lhsT=wq[:, k, oc * 128:(oc + 1) * 128], rhs=h[:, k, :],
                                           start=(k == 0), stop=(k == KC - 1))
                        return ins
                    T.add("pe", mmq, r=[hb] + wqb, w=[self.pb[bk]])
                    T.add("act", lambda e, oc=oc, bk=bk, qt=qt: e.activation(out=qt[:, oc, :], in_=self.psum[:, bk, 0:TT], func=AF.Identity, scale=0.125),
                          r=[self.pb[bk]], w=[qb])
                return (r0, t0, xt, xb, qt, qb)

            cur = prologue(0)
            for it in range(ntile):
                r0, t0, xt, xb, qt, qb = cur
                ot, ob = orr.next()
                units = []
                for p in range(TT // (2 * GW)):
                    r = r0 + 2 * p
                    k0, nch, kind, pidx = self.na_plan(group, r)
                    for hp in range(NH // 2):
                        units.append((p, r, k0, nch, kind, pidx, hp))
                pend = None
                pend_dve = None
                for ui, (p, r, k0, nch, kind, pidx, hp) in enumerate(units):
                    if ui == len(units) // 2 and it + 1 < ntile:
                        cur = prologue(it + 1)
                    qs = p * 128
                    pt, ptb = ptr.next()
                    ptbs = (ptb, ptb2[(ptr.i - 1) % 2])
                    psm, psmb = sumr.next()
                    psmbs = (psmb, psmb2[(sumr.i - 1) % 2])
                    bo = 4 + (ui % 2)
                    psO = self.psum[:, bo, :]
                    for par in range(2):
                        hd = 2 * hp + par
                        hs = par * 64
                        sb0 = (ucount % 2) * 2
                        ucount += 1
                        psS = self.psum[:, sb0:sb0 + 2, :].rearrange("p b n -> p (b n)")

                        def mms(e, psS=psS, k0=k0, nch=nch, r=r, hp=hp, hs=hs, hd=hd, qt=qt, qs=qs, kind=kind):
                            if kind == "interior":
                                c0 = 7 * 128
                            else:
                                c0 = ((k0 - r + 7 - 1) // 2) * 128
                            n1 = min(nch, 4) * 128
                            e.matmul(psS[:, 0:n1], lhsT=self.identb[:, :], rhs=bt[:, hd, c0:c0 + n1], start=True, stop=False)
                            for c in range(min(nch, 4)):
                                ktok = (k0 + 2 * c - kr0) * GW
                                ins = e.matmul(psS[:, c * 128:(c + 1) * 128], lhsT=KT[hs:hs + 64, hp, ktok:ktok + 128], rhs=qt[hs:hs + 64, hp, qs:qs + 128],
                                               start=False, stop=(c == min(nch, 4) - 1))
                            if nch > 4:
                                e.matmul(psS[:, 512:nch * 128], lhsT=self.identb[:, :], rhs=bt[:, hd, c0 + 512:c0 + nch * 128], start=True, stop=False)
                                for c in range(4, nch):
                                    ktok = (k0 + 2 * c - kr0) * GW
                                    ins = e.matmul(psS[:, c * 128:(c + 1) * 128], lhsT=KT[hs:hs + 64, hp, ktok:ktok + 128], rhs=qt[hs:hs + 64, hp, qs:qs + 128],
                                                   start=False, stop=(c == nch - 1))
                            return ins
                        T.add("pe", mms, r=[qb, btb[hd]], w=[self.pb[sb0], self.pb[sb0 + 1]])
                        if par == 0 and pend is not None:
                            pend_dve = pend()
                            pend = None
                        if kind != "boundary":
                            def ex0(e, pt=pt, psS=psS, nch=nch, par=par):
                                ins = e.activation(out=pt[:, par, 0:512], in_=psS[:, 0:512], func=AF.Exp)
                                if nch > 4:
                                    ins = e.activation(out=pt[:, par, 512:nch * 128], in_=psS[:, 512:nch * 128], func=AF.Exp)
                                return ins
                            T.add("act", ex0, r=[self.pb[sb0], self.pb[sb0 + 1]], w=[ptbs[par]])
                        else:
                            def ex(e, pt=pt, psS=psS, pidx=pidx, par=par):
                                for c in range(6):
                                    for jp in range(2):
                                        o = c * 128 + jp * 64
                                        ins = e.activation(out=pt[:, par, o:o + 64], in_=psS[:, o:o + 64], func=AF.Exp,
                                                           bias=self.pcol("nab", (pidx * 6 + c) * 2 + jp))
                                return ins
                            T.add("act", ex, r=[self.pb[sb0], self.pb[sb0 + 1]], w=[ptbs[par]])


                    def finish(pt=pt, ptbs=ptbs, psm=psm, psmbs=psmbs, psO=psO, bo=bo, k0=k0, nch=nch, hp=hp, ot=ot, ob=ob, qs=qs):
                        def mmo(e):
                            for c in range(nch):
                                vt = (k0 + 2 * c - kr0) // 2
                                ins = e.matmul(psO[:, 0:256], lhsT=V[:, vt, hp * 128:(hp + 1) * 128], rhs=pt[:, :, c * 128:(c + 1) * 128],
                                               start=(c == 0), stop=(c == nch - 1))
                            return ins
                        T.add("pe", mmo, r=list(ptbs), w=[self.pb[bo]])

                        def mmd(e):
                            for c in range(nch):
                                ins = e.matmul(psO[:, 256:512], lhsT=self.onesb[:, :], rhs=pt[:, :, c * 128:(c + 1) * 128],
                                               start=(c == 0), stop=(c == nch - 1))
                            return ins
                        T.add("pe", mmd, r=list(ptbs), w=[self.pb[bo]])

                        def finish_dve():
                            rd, rdb = rdr.next()
                            T.add("dve", lambda e: e.reciprocal(out=rd[:, :], in_=psO[:, 256:512]), r=[self.pb[bo]], w=[rdb])

                            def nrm(e):
                                e.tensor_tensor(out=ot[0:64, hp, qs:qs + 128], in0=psO[0:64, 0:128], in1=rd[0:64, 0:128], op=ALU.mult)
                                return e.tensor_tensor(out=ot[64:128, hp, qs:qs + 128], in0=psO[64:128, 128:256], in1=rd[64:128, 128:256], op=ALU.mult)
                            T.add("dve", nrm, r=[self.pb[bo], rdb], w=[ob])
                        return finish_dve
                    if pend_dve is not None:
                        pend_dve()
                        pend_dve = None
                    pend = finish
                if pend is not None:
                    pend_dve = pend()
                    pend = None
                if pend_dve is not None:
                    pend_dve()
                    pend_dve = None
                for oc in range(KC):
                    bk = 6 + (oc % 2)

                    def mmy(e, oc=oc, bk=bk, ot=ot):
                        for k in range(KC):
                            ins = e.matmul(self.psum[:, bk, 0:TT], lhsT=wo[:, k, oc * 128:(oc + 1) * 128], rhs=ot[:, k, :],
                                           start=(k == 0), stop=(k == KC - 1))
                        return ins
                    T.add("pe", mmy, r=[ob] + wob, w=[self.pb[bk]])
                    T.add("dve", lambda e, oc=oc, bk=bk, xt=xt: e.tensor_tensor(out=xt[:, oc, :], in0=xt[:, oc, :], in1=self.psum[:, bk, 0:TT], op=ALU.add),
                          r=[self.pb[bk], xb], w=[xb])
                T.add("pool", lambda e, xt=xt, t0=t0, dst=self.xT2: e.dma_start(out=dst[:, :, t0:t0 + TT].rearrange("c p t -> p c t"), in_=xt[:, :, :]),
                      r=[xb], w=self.xblocks(self.xbuf2, t0, TT), dma=True)
            T.barrier()


    def lru_prep(self):
        T = self.T
        lc = self.lruc
        for d in range(2):
            lam = self.pcol(f"lam{d}", 0, LC)
            T.add("act", lambda e, d=d, lam=lam: e.activation(out=lc[:, d, 0, :], in_=lam, func=AF.Exp, scale=-1.0), r=[self.pbuf], w=[self.pbuf])
            T.add("act", lambda e, d=d: e.activation(out=lc[:, d, 0, :], in_=lc[:, d, 0, :], func=AF.Ln, bias=self.pcol("one")), r=[self.pbuf], w=[self.pbuf])
            T.add("dve", lambda e, d=d: e.tensor_scalar(out=lc[:, d, 1, :], in0=lc[:, d, 0, :], scalar1=-4.0, scalar2=None, op0=ALU.mult), r=[self.pbuf], w=[self.pbuf])
            T.add("dve", lambda e, d=d: e.tensor_scalar(out=lc[:, d, 0, :], in0=lc[:, d, 0, :], scalar1=-8.0, scalar2=None, op0=ALU.mult), r=[self.pbuf], w=[self.pbuf])
            T.add("dve", lambda e, d=d: e.tensor_scalar(out=lc[:, d, 2, :], in0=self.pcol(f"ba{d}", 0, LC), scalar1=0.5, scalar2=None, op0=ALU.mult), r=[self.pbuf], w=[self.pbuf])
            T.add("dve", lambda e, d=d: e.tensor_scalar(out=lc[:, d, 3, :], in0=self.pcol(f"bx{d}", 0, LC), scalar1=0.5, scalar2=None, op0=ALU.mult), r=[self.pbuf], w=[self.pbuf])
        T.barrier()

    def phase_lru(self, li):
        self.lru_prep()
        self.lru_l1(li)
        self.lru_l23(li, 0)
        self.lru_l23(li, 1)

    def lru_l1(self, li):
        T, cfg = self.T, self.cfg
        A = self.arena
        A.reset()
        SEG, NT = cfg.SEG, cfg.NT
        TT, W = 256, 259
        gname = f"nmix{li}"
        win = A.alloc([128, KC, 2 * LRU_C], BF16, "win")
        winb = [Buf() for _ in range(KC)]
        self.load_w(win, winb, self.lru_win, KC)
        xr_ = Ring(A, 2, [128, KC, W], F32, "x")
        hr = Ring(A, 2, [128, KC, W], BF16, "h")
        sqr = Ring(A, 1, [128, KC, W], BF16, "sq")
        rs = Ring(A, 2, [128, W], F32, "rs")
        xrr = Ring(A, 2, [128, LC, W], F32, "xr")
        xcr = Ring(A, 2, [128, LC, TT], F32, "xc")
        ggr = Ring(A, 2, [128, LC, TT], F32, "gg")
        tmr = Ring(A, 3, [128, TT], F32, "tm")
        groups = [(0, 2 * SEG), (2 * SEG, 3 * SEG)]

        def prologue(it):
            t0 = it * TT
            g0, g1 = groups[0] if t0 < 2 * SEG else groups[1]
            lo, hi = max(t0 - 2, g0), min(t0 + TT + 1, g1)
            xt, xb = xr_.next()
            if lo > t0 - 2:
                T.add("dve", lambda e, xt=xt: e.memset(xt[:, :, 0:2], 0.0), w=[xb])
            if hi < t0 + TT + 1:
                T.add("dve", lambda e, xt=xt: e.memset(xt[:, :, W - 1:W], 0.0), w=[xb])
            c0 = lo - (t0 - 2)
            T.add("sp", lambda e, xt=xt, lo=lo, hi=hi, c0=c0, src=self.xT: e.dma_start(
                out=xt[:, :, c0:c0 + hi - lo], in_=src[:, :, lo:hi].rearrange("c p t -> p c t")),
                r=self.xblocks(self.xbuf, lo, hi - lo), w=[xb], dma=True)
            if t0 == SEG:
                T.add("dve", lambda e, xt=xt: e.tensor_scalar(out=xt[:, :, 0:2], in0=xt[:, :, 0:2], scalar1=self.pcol("flag"), scalar2=None, op0=ALU.mult),
                      r=[xb], w=[xb])
            if t0 + TT == SEG:
                T.add("dve", lambda e, xt=xt: e.tensor_scalar(out=xt[:, :, W - 1:W], in0=xt[:, :, W - 1:W], scalar1=self.pcol("flag"), scalar2=None, op0=ALU.mult),
                      r=[xb], w=[xb])
            h, hb = hr.next()
            sq, sqb = sqr.next()
            rstd, rb = rs.next()
            self.rmsnorm(xt, xb, KC, W, gname, h, hb, sq, sqb, rstd, rb, D)
            return h, hb

        nxt = prologue(0)
        for it in range(NT // TT):
            t0 = it * TT
            h, hb = nxt
            gg, ggb = ggr.next()
            xr, xrb = xrr.next()
            xc, xcb_ = xcr.next()
            pendB = None
            for j in range(2 * LC):
                bk = self.bank()
                isg = j < LC
                c_lo, c_n = (2, TT) if isg else (0, W)

                def mm(e, j=j, bk=bk, h=h, c_lo=c_lo, c_n=c_n):
                    for k in range(KC):
                        ins = e.matmul(self.psum[:, bk, 0:c_n], lhsT=win[:, k, j * 128:(j + 1) * 128], rhs=h[:, k, c_lo:c_lo + c_n],
                                       start=(k == 0), stop=(k == KC - 1))
                    return ins
                T.add("pe", mm, r=[hb] + winb, w=[self.pb[bk]])
                ps = self.psum[:, bk, 0:TT]
                if isg:
                    tm, tmb = tmr.next()
                    T.add("act", lambda e, tm=tm, ps=ps: e.activation(out=tm[:, :], in_=ps, func=AF.Square, scale=float(GELU_C2 ** 0.5)), r=[self.pb[bk]], w=[tmb])
                    T.add("dve", lambda e, tm=tm, ps=ps: e.scalar_tensor_tensor(out=tm[:, :], in0=tm[:, :], scalar=GELU_C1, in1=ps, op0=ALU.add, op1=ALU.mult),
                          r=[tmb, self.pb[bk]], w=[tmb])
                    if pendB is not None:
                        pendB()

                    def stageB(tm=tm, tmb=tmb, ps=ps, bk=bk, j=j, gg=gg, ggb=ggb):
                        T.add("act", lambda e: e.activation(out=tm[:, :], in_=tm[:, :], func=AF.Tanh), r=[tmb], w=[tmb])
                        T.add("dve", lambda e: e.scalar_tensor_tensor(out=gg[:, j, :], in0=tm[:, :], scalar=1.0, in1=ps, op0=ALU.add, op1=ALU.mult),
                              r=[tmb, self.pb[bk]], w=[ggb])
                    pendB = stageB
                    if j == LC - 1:
                        pendB()
                        pendB = None
                        if it + 1 < NT // TT:
                            nxt = prologue(it + 1)
                else:
                    T.add("act", lambda e, xr=xr, j=j, bk=bk: e.activation(out=xr[:, j - LC, :], in_=self.psum[:, bk, 0:W], func=AF.Copy),
                          r=[self.pb[bk]], w=[xrb])
            def conv0(e, xr=xr, xc=xc):
                for c in range(LC):
                    ins = e.activation(out=xc[:, c, :], in_=xr[:, c, 0:TT], func=AF.Identity, scale=self.pcol("convw", c * 4), bias=self.pcol("convb", c))
                return ins
            T.add("act", conv0, r=[xrb], w=[xcb_])
            for k in range(1, 4):
                def convk(e, xr=xr, xc=xc, k=k):
                    for c in range(LC):
                        ins = e.scalar_tensor_tensor(out=xc[:, c, :], in0=xr[:, c, k:k + TT], scalar=self.pcol("convw", c * 4 + k), in1=xc[:, c, :],
                                                     op0=ALU.mult, op1=ALU.add)
                    return ins
                T.add("dve", convk, r=[xrb, xcb_], w=[xcb_])
            T.add("pool", lambda e, xc=xc, t0=t0: e.dma_start(out=self.xc_s[:, :, t0:t0 + TT].rearrange("c p t -> p c t"), in_=xc[:, :, :]),
                  r=[xcb_], w=self.xblocks(self.xcbuf, t0, TT), dma=True)
            T.add("pool", lambda e, gg=gg, t0=t0: e.dma_start(out=self.gg_s[:, :, t0:t0 + TT].rearrange("c p t -> p c t"), in_=gg[:, :, :]),
                  r=[ggb], w=self.xblocks(self.ggbuf, t0, TT), dma=True)
        T.barrier()

    def lru_l23(self, li, d):
        T, cfg = self.T, self.cfg
        A = self.arena
        A.reset()
        SEG, NT = cfg.SEG, cfg.NT
        TT = 256
        lc = self.lruc
        band = A.alloc([128, 2, LC * 3 * 128], BF16, "band")
        bandb = [Buf(), Buf()]
        for i in range(2):
            T.add("pool", lambda e, i=i: e.dma_start(out=band[:, i, :], in_=self.lru_band[2 * d + i]), w=[bandb[i]], dma=True)
        if d == 1:
            wout = A.alloc([128, LC, D], BF16, "wout")
            woutb = [Buf() for _ in range(LC)]
            self.load_w(wout, woutb, self.lru_wout, LC)
            hfr = Ring(A, 1, [128, LC, TT], F32, "hf")
            ggr = Ring(A, 1, [128, LC, TT], F32, "gg")
            xr_ = Ring(A, 1, [128, KC, TT], F32, "x")
            ybr = Ring(A, 1, [128, LC, TT], BF16, "yb")
        xcr = Ring(A, 2, [128, LC, TT], F32, "xc")
        xbr = Ring(A, 1, [128, LC, TT], BF16, "xcb")
        cfull = A.alloc([128, LC, TT], F32, "cfull")
        cfb = Buf("cfull")
        T.add("dve", lambda e: e.memset(cfull[:, :, :], 1.0), w=[cfb])

        def cfill(e):
            for m in range(LC):
                ins = e.tensor_scalar(out=cfull[:, m, :], in0=cfull[:, m, :], scalar1=lc[:, d, 0, m:m + 1], scalar2=None, op0=ALU.mult)
            return ins
        T.add("dve", cfill, r=[cfb, self.pbuf], w=[cfb])
        ar = Ring(A, 2, [128, LC, TT], F32, "a")
        mr = Ring(A, 2, [128, LC, TT], F32, "m")
        tir = Ring(A, 2, [128, LC, TT], F32, "ti")
        hr = Ring(A, 2 if d == 0 else 1, [128, LC, TT], F32, "hh")
        st = A.alloc([128, LC], F32, "st")
        stb = Buf("st")
        ntile = NT // TT
        order = list(range(ntile)) if d == 0 else list(range(ntile - 1, -1, -1))
        groups = [(0, 2 * SEG), (2 * SEG, 3 * SEG)]
        for it in order:
            t0 = it * TT
            g0, g1 = groups[0] if t0 < 2 * SEG else groups[1]
            first = (t0 == g0) if d == 0 else (t0 + TT == g1)
            if first:
                T.add("dve", lambda e: e.memset(st[:, :], 0.0), w=[stb])
            xc, xcb_ = xcr.next()
            T.add("sp", lambda e, xc=xc, t0=t0: e.dma_start(out=xc[:, :, :], in_=self.xc_s[:, :, t0:t0 + TT].rearrange("c p t -> p c t")),
                  r=self.xblocks(self.xcbuf, t0, TT), w=[xcb_], dma=True)
            if d == 1:
                hf, hfb = hfr.next()
                gg, ggb = ggr.next()
                xt, xb = xr_.next()
                T.add("sp", lambda e, hf=hf, t0=t0: e.dma_start(out=hf[:, :, :], in_=self.hf_s[:, :, t0:t0 + TT].rearrange("c p t -> p c t")),
                      r=self.xblocks(self.hfbuf, t0, TT), w=[hfb], dma=True)
                T.add("sp", lambda e, gg=gg, t0=t0: e.dma_start(out=gg[:, :, :], in_=self.gg_s[:, :, t0:t0 + TT].rearrange("c p t -> p c t")),
                      r=self.xblocks(self.ggbuf, t0, TT), w=[ggb], dma=True)
                T.add("sp", lambda e, xt=xt, t0=t0, src=self.xT: e.dma_start(out=xt[:, :, :], in_=src[:, :, t0:t0 + TT].rearrange("c p t -> p c t")),
                      r=self.xblocks(self.xbuf, t0, TT), w=[xb], dma=True)
            xcbf, xcbfb = xbr.next()
            T.add("act", lambda e, xc=xc, xcbf=xcbf: e.activation(out=xcbf[:, :, :], in_=xc[:, :, :], func=AF.Copy), r=[xcb_], w=[xcbfb])
            a, ab = ar.next()
            mm_, mb = mr.next()
            ti, tib = tir.next()
            hh, hhb = hr.next()
            for m in range(LC):
                ba_, bx_ = self.bank(), self.bank()
                kks = [kk for kk in range(3) if 0 <= m + kk - 1 < LC]

                def mg(e, m=m, ba_=ba_, bx_=bx_, kks=kks, xcbf=xcbf):
                    for i, bk in ((0, ba_), (1, bx_)):
                        for n, kk in enumerate(kks):
                            o = (m * 3 + kk) * 128
                            ins = e.matmul(self.psum[:, bk, 0:TT], lhsT=band[:, i, o:o + 128], rhs=xcbf[:, m + kk - 1, :],
                                           start=(n == 0), stop=(n == len(kks) - 1))
                    return ins
                T.add("pe", mg, r=[xcbfb] + bandb, w=[self.pb[ba_], self.pb[bx_]])
                T.add("act", lambda e, a=a, ba_=ba_, m=m: e.activation(out=a[:, m, :], in_=self.psum[:, ba_, 0:TT], func=AF.Tanh, scale=0.5,
                                                                        bias=lc[:, d, 2, m:m + 1]), r=[self.pb[ba_], self.pbuf], w=[ab])
                T.add("act", lambda e, ti=ti, bx_=bx_, m=m: e.activation(out=ti[:, m, :], in_=self.psum[:, bx_, 0:TT], func=AF.Tanh, scale=0.5,
                                                                          bias=lc[:, d, 3, m:m + 1]), r=[self.pb[bx_]], w=[tib])
            T.add("dve", lambda e, a=a: e.scalar_tensor_tensor(out=a[:, :, :], in0=a[:, :, :], scalar=1.0, in1=cfull[:, :, :], op0=ALU.add, op1=ALU.mult),
                  r=[ab, cfb], w=[ab])
            T.add("act", lambda e, a=a, mm_=mm_: e.activation(out=mm_[:, :, :], in_=a[:, :, :], func=AF.Exp), r=[ab], w=[mb])
            T.add("act", lambda e, a=a: e.activation(out=a[:, :, :], in_=a[:, :, :], func=AF.Exp, scale=0.5), r=[ab], w=[ab])
            T.add("act", lambda e, mm_=mm_: e.activation(out=mm_[:, :, :], in_=mm_[:, :, :], func=AF.Sqrt, scale=-1.0, bias=self.pcol("one")), r=[mb], w=[mb])
            T.add("dve", lambda e, ti=ti, xc=xc: e.scalar_tensor_tensor(out=ti[:, :, :], in0=ti[:, :, :], scalar=1.0, in1=xc[:, :, :],
                                                                        op0=ALU.add, op1=ALU.mult), r=[tib, xcb_], w=[tib])
            T.add("dve", lambda e, ti=ti, mm_=mm_: e.scalar_tensor_tensor(out=ti[:, :, :], in0=ti[:, :, :], scalar=0.5, in1=mm_[:, :, :],
                                                                          op0=ALU.mult, op1=ALU.mult), r=[tib, mb], w=[tib])

            def scan(e, a=a, ti=ti, hh=hh):
                for m in range(LC):
                    if d == 0:
                        ins = e.tensor_tensor_scan(out=hh[:, m, :], data0=a[:, m, :], data1=ti[:, m, :], initial=st[:, m:m + 1],
                                                   op0=ALU.mult, op1=ALU.add)
                    else:
                        ins = e.tensor_tensor_scan(out=hh[:, m, ::-1], data0=a[:, m, ::-1], data1=ti[:, m, ::-1], initial=st[:, m:m + 1],
                                                   op0=ALU.mult, op1=ALU.add)
                return ins
            T.add("dve", scan, r=[ab, tib, stb], w=[hhb])
            col = TT - 1 if d == 0 else 0
            crossing = (t0 + TT == SEG) if d == 0 else (t0 == SEG)
            if crossing:
                T.add("dve", lambda e, hh=hh, col=col: e.tensor_scalar(out=st[:, :], in0=hh[:, :, col], scalar1=self.pcol("flag"), scalar2=None, op0=ALU.mult),
                      r=[hhb], w=[stb])
            else:
                T.add("dve", lambda e, hh=hh, col=col: e.tensor_copy(out=st[:, :], in_=hh[:, :, col]), r=[hhb], w=[stb])
            if d == 0:
                T.add("pool", lambda e, hh=hh, t0=t0: e.dma_start(out=self.hf_s[:, :, t0:t0 + TT].rearrange("c p t -> p c t"), in_=hh[:, :, :]),
                      r=[hhb], w=self.xblocks(self.hfbuf, t0, TT), dma=True)
                continue
            yb, ybb = ybr.next()
            T.add("dve", lambda e, hf=hf, hh=hh: e.tensor_tensor(out=hf[:, :, :], in0=hf[:, :, :], in1=hh[:, :, :], op=ALU.add), r=[hfb, hhb], w=[hfb])
            T.add("dve", lambda e, hf=hf, gg=gg, yb=yb: e.scalar_tensor_tensor(out=yb[:, :, :], in0=hf[:, :, :], scalar=0.5, in1=gg[:, :, :],
                                                                                 op0=ALU.mult, op1=ALU.mult), r=[hfb, ggb], w=[ybb])
            for oc in range(KC):
                bk = self.bank()

                def mo(e, oc=oc, bk=bk, yb=yb):
                    for k in range(LC):
                        ins = e.matmul(self.psum[:, bk, 0:TT], lhsT=wout[:, k, oc * 128:(oc + 1) * 128], rhs=yb[:, k, :],
                                       start=(k == 0), stop=(k == LC - 1))
                    return ins
                T.add("pe", mo, r=[ybb] + woutb, w=[self.pb[bk]])
                T.add("dve", lambda e, oc=oc, bk=bk, xt=xt: e.tensor_tensor(out=xt[:, oc, :], in0=xt[:, oc, :], in1=self.psum[:, bk, 0:TT], op=ALU.add),
                      r=[self.pb[bk], xb], w=[xb])
            T.add("pool", lambda e, xt=xt, t0=t0, dst=self.xT: e.dma_start(out=dst[:, :, t0:t0 + TT].rearrange("c p t -> p c t"), in_=xt[:, :, :]),
                  r=[xb], w=self.xblocks(self.xbuf, t0, TT), dma=True)
        T.barrier()


    def phase_mla(self, li):
        cfg = self.cfg
        SEG = cfg.SEG
        for (g0, g1) in ((0, 2 * SEG), (2 * SEG, 3 * SEG)):
            self.mla_group(li, g0, g1)
        self.mla_m3(li)

    def mla_group(self, li, g0, g1):
        T, cfg = self.T, self.cfg
        A = self.arena
        A.reset()
        SEG = cfg.SEG
        TT = 512
        TG = g1 - g0
        gname = f"nmix{li}"
        cqT = A.alloc([128, 3, TG], BF16, "cqT")
        ckvT = A.alloc([128, 2, TG], BF16, "ckvT")
        KR = A.alloc([128, TG], BF16, "KR")
        mark = A.off
        wdq = A.alloc([128, KC, 384], BF16, "wdq")
        wdkv = A.alloc([128, KC, 256], BF16, "wdkv")
        wkr0 = A.alloc([128, KC, 96], BF16, "wkr0")
        wkr1 = A.alloc([128, KC, 96], BF16, "wkr1")
        wdqb = [Buf() for _ in range(KC)]
        wdkvb = [Buf() for _ in range(KC)]
        wkr0b = [Buf() for _ in range(KC)]
        wkr1b = [Buf() for _ in range(KC)]
        self.load_w(wdq, wdqb, self.mla_wdq, KC)
        self.load_w(wdkv, wdkvb, self.mla_wdkv, KC)
        self.load_w(wkr0, wkr0b, self.mla_wkr[0], KC)
        self.load_w(wkr1, wkr1b, self.mla_wkr[1], KC)
        xr = Ring(A, 2, [128, KC, TT], F32, "x")
        hr = Ring(A, 1, [128, KC, TT], BF16, "h")
        sqr = Ring(A, 1, [128, KC, TT], BF16, "sq")
        rs = Ring(A, 2, [128, TT], F32, "rs")
        cqfr = Ring(A, 1, [128, 3, TT], F32, "cqf")
        ckvfr = Ring(A, 1, [128, 2, TT], F32, "ckvf")
        ropr = Ring(A, 2, [128, 2, TT], F32, "rope")
        t12r = Ring(A, 2, [128, 2, TT], F32, "t12")
        resb = Buf("mla_res")
        for it in range(TG // TT):
            tl = it * TT
            t0 = g0 + tl
            xt, xb = xr.next()
            T.add("sp", lambda e, xt=xt, t0=t0, src=self.xT: e.dma_start(out=xt[:, :, :], in_=src[:, :, t0:t0 + TT].rearrange("c p t -> p c t")),
                  r=self.xblocks(self.xbuf, t0, TT), w=[xb], dma=True)
            rp, rpb = ropr.next()
            for i in range(2):
                T.add("sp", lambda e, rp=rp, i=i, t0=t0: e.dma_start(out=rp[64:96, i, :], in_=self.rope_d[i][:, t0:t0 + TT]), w=[rpb], dma=True)
            h, hb = hr.next()
            sq, sqb = sqr.next()
            rstd, rb = rs.next()
            self.rmsnorm(xt, xb, KC, TT, gname, h, hb, sq, sqb, rstd, rb, D)
            cqf, cqfb = cqfr.next()
            ckvf, ckvfb = ckvfr.next()
            for (w_, wb_, n, dstf, dstb) in ((wdq, wdqb, 3, cqf, cqfb), (wdkv, wdkvb, 2, ckvf, ckvfb)):
                for oc in range(n):
                    bk = self.bank()

                    def mm(e, w_=w_, oc=oc, bk=bk, h=h):
                        for k in range(KC):
                            ins = e.matmul(self.psum[:, bk, :], lhsT=w_[:, k, oc * 128:(oc + 1) * 128], rhs=h[:, k, :], start=(k == 0), stop=(k == KC - 1))
                        return ins
                    T.add("pe", mm, r=[hb] + wb_, w=[self.pb[bk]])
                    T.add("act", lambda e, dstf=dstf, oc=oc, bk=bk: e.activation(out=dstf[:, oc, :], in_=self.psum[:, bk, :], func=AF.Copy),
                          r=[self.pb[bk]], w=[dstb])
            sq2, sq2b = sqr.next()
            rstd2, rb2 = rs.next()
            self.rmsnorm(cqf, cqfb, 3, TT, "gq", cqT[:, :, tl:tl + TT], resb, sq2, sq2b, rstd2, rb2, 384)
            sq3, sq3b = sqr.next()
            rstd3, rb3 = rs.next()
            self.rmsnorm(ckvf, ckvfb, 2, TT, "gkv", ckvT[:, :, tl:tl + TT], resb, sq3, sq3b, rstd3, rb3, 256)
            ba_, bb_ = self.bank(), self.bank()

            def mkr(e, ba_=ba_, bb_=bb_, h=h):
                for (w_, bk) in ((wkr0, ba_), (wkr1, bb_)):
                    for k in range(KC):
                        ins = e.matmul(self.psum[0:96, bk, :], lhsT=w_[:, k, :], rhs=h[:, k, :], start=(k == 0), stop=(k == KC - 1))
                return ins
            T.add("pe", mkr, r=[hb] + wkr0b + wkr1b, w=[self.pb[ba_], self.pb[bb_]])
            t12, t12b = t12r.next()
            T.add("dve", lambda e, t12=t12, ba_=ba_, rp=rp: e.tensor_tensor(out=t12[64:96, 0, :], in0=self.psum[64:96, ba_, :], in1=rp[64:96, 0, :], op=ALU.mult),
                  r=[self.pb[ba_], rpb], w=[t12b])
            T.add("dve", lambda e, t12=t12, bb_=bb_, rp=rp: e.tensor_tensor(out=t12[64:96, 1, :], in0=self.psum[64:96, bb_, :], in1=rp[64:96, 1, :], op=ALU.mult),
                  r=[self.pb[bb_], rpb], w=[t12b])
            T.add("dve", lambda e, t12=t12, tl=tl: e.tensor_tensor(out=KR[64:96, tl:tl + TT], in0=t12[64:96, 0, :], in1=t12[64:96, 1, :], op=ALU.add),
                  r=[t12b], w=[resb])
        T.barrier()
        A.off = mark
        wuq = A.alloc([128, 3, 1536], BF16, "wuq")
        wuqs = A.alloc([128, 3, 1536], BF16, "wuqs")
        wuk = A.alloc([128, 2, 1024], BF16, "wuk")
        wuv = A.alloc([128, 2, 1024], BF16, "wuv")
        wuqb = [Buf() for _ in range(3)]
        wuqsb = [Buf() for _ in range(3)]
        wukb = [Buf() for _ in range(2)]
        wuvb = [Buf() for _ in range(2)]
        self.load_w(wuk, wukb, self.mla_wuk, 2)
        self.load_w(wuv, wuvb, self.mla_wuv, 2)
        self.load_w(wuq, wuqb, self.mla_wuq[0], 3)
        self.load_w(wuqs, wuqsb, self.mla_wuq[1], 3)
        NKC = TG // 128
        Kg = A.alloc([128, 4, TG], BF16, "Kg")
        Vg = A.alloc([128, NKC, 4, 128], BF16, "Vg")
        kgm = [Buf(f"kgm{i}") for i in range(4)]
        kgr = [Buf(f"kgr{i}") for i in range(4)]
        vgs = [Buf("vg0"), Buf("vg1")]
        qhr = Ring(A, 2, [128, TT], BF16, "qh")
        ropr = Ring(A, 2, [128, 2, TT], F32, "rope")
        t12r = Ring(A, 2, [128, 2, TT], F32, "t12")
        ptr = Ring(A, 4, [128, TT], BF16, "pt")
        rdr = Ring(A, 2, [128, TT], F32, "rdn")
        bcr = Ring(A, 2, [128, TT], F32, "bcs")
        onr = Ring(A, 3, [128, TT], BF16, "on")
        T.add("dve", lambda e: e.memset(Vg[:, :, :, 64:128], 1.0), w=vgs)
        T.add("pool", lambda e: e.memset(Kg[64:128, :, :], 0.0), w=kgr)
        for (qh_, qhb_) in zip(qhr.tiles, qhr.bufs):
            T.add("pool", lambda e, qh_=qh_: e.memset(qh_[64:128, :], 0.0), w=[qhb_])
        scale = float(96 ** -0.5)
        st_ = {"sbank": 0, "obank": 0}
        for g in range(4):
            for it in range(TG // TT):
                tl = it * TT
                for hh in range(4):
                    hd = 4 * g + hh
                    bk = self.bank()

                    def mk(e, hd=hd, bk=bk, tl=tl):
                        for k in range(2):
                            ins = e.matmul(self.psum[0:64, bk, :], lhsT=wuk[:, k, hd * 64:(hd + 1) * 64], rhs=ckvT[:, k, tl:tl + TT], start=(k == 0), stop=(k == 1))
                        return ins
                    T.add("pe", mk, r=wukb, w=[self.pb[bk]])
                    if False:
                        pass
                    else:
                        T.add("dve", lambda e, hh=hh, bk=bk, tl=tl: e.tensor_copy(out=Kg[0:64, hh, tl:tl + TT], in_=self.psum[0:64, bk, :]),
                              r=[self.pb[bk]], w=[kgm[hh]])
                    T.add("pool", lambda e, hh=hh, tl=tl: e.tensor_copy(out=Kg[64:96, hh, tl:tl + TT], in_=KR[64:96, tl:tl + TT]), w=[kgr[hh]])
                for st in range(TT // 128):
                    kc = (tl // 128) + st
                    bk = self.bank()

                    def mv(e, kc=kc, bk=bk, g=g):
                        for k in range(2):
                            ins = e.matmul(self.psum[:, bk, 0:256], lhsT=ckvT[:, k, kc * 128:(kc + 1) * 128], rhs=wuv[:, k, g * 256:(g + 1) * 256],
                                           start=(k == 0), stop=(k == 1))
                        return ins
                    T.add("pe", mv, r=wuvb, w=[self.pb[bk]])
                    src = self.psum[:, bk, 0:256].rearrange("p (a b) -> p a b", a=4)
                    if False:
                        pass
                    else:
                        T.add("dve", lambda e, kc=kc, src=src: e.tensor_copy(out=Vg[:, kc, :, 0:64], in_=src), r=[self.pb[bk]], w=[vgs[1]])
            units = [(it, hh) for it in range(TG // TT) for hh in range(4)]
            rope_cache = {}

            def prep(u, g=g):
                it, hh = units[u]
                tl = it * TT
                t0 = g0 + tl
                hd = 4 * g + hh
                if it not in rope_cache:
                    rp, rpb = ropr.next()
                    for i in range(2):
                        T.add("sp", lambda e, rp=rp, i=i, t0=t0: e.dma_start(out=rp[64:96, i, :], in_=self.rope_d[i][:, t0:t0 + TT]), w=[rpb], dma=True)
                    rope_cache.clear()
                    rope_cache[it] = (rp, rpb)
                rp, rpb = rope_cache[it]

                def mq(e, hd=hd, tl=tl):
                    for (w_, bk) in ((wuq, 5), (wuqs, 6)):
                        for k in range(3):
                            ins = e.matmul(self.psum[0:96, bk, :], lhsT=w_[:, k, hd * 96:(hd + 1) * 96], rhs=cqT[:, k, tl:tl + TT], start=(k == 0), stop=(k == 2))
                    return ins
                T.add("pe", mq, r=wuqb + wuqsb, w=[self.pb[5], self.pb[6]])
                qh, qhb = qhr.next()
                t12, t12b = t12r.next()
                T.add("dve", lambda e, qh=qh: e.tensor_copy(out=qh[0:64, :], in_=self.psum[0:64, 5, :]), r=[self.pb[5]], w=[qhb])
                T.add("dve", lambda e, t12=t12, rp=rp: e.tensor_tensor(out=t12[64:96, 0, :], in0=self.psum[64:96, 5, :], in1=rp[64:96, 0, :], op=ALU.mult),
                      r=[self.pb[5], rpb], w=[t12b])
                T.add("dve", lambda e, t12=t12, rp=rp: e.tensor_tensor(out=t12[64:96, 1, :], in0=self.psum[64:96, 6, :], in1=rp[64:96, 1, :], op=ALU.mult),
                      r=[self.pb[6], rpb], w=[t12b])
                T.add("dve", lambda e, t12=t12, qh=qh: e.tensor_tensor(out=qh[64:96, :], in0=t12[64:96, 0, :], in1=t12[64:96, 1, :], op=ALU.add),
                      r=[t12b], w=[qhb])
                return qh, qhb

            def finish_head(bo, hd, t0):
                rdn, rdb = rdr.next()
                bcs, bcb = bcr.next()
                on, onb = onr.next()
                T.add("dve", lambda e, rdn=rdn, bo=bo: e.reciprocal(out=rdn[64:65, :], in_=self.psum[64:65, bo, :]), r=[self.pb[bo]], w=[rdb])

                def rest():
                    T.add("pe", lambda e, rdn=rdn: e.matmul(self.psum[0:64, 7, :], lhsT=self.onesf[64:65, 0:64], rhs=rdn[64:65, :], start=True, stop=True),
                          r=[rdb], w=[self.pb[7]])
                    T.add("dve", lambda e, bcs=bcs: e.tensor_copy(out=bcs[0:64, :], in_=self.psum[0:64, 7, :]), r=[self.pb[7]], w=[bcb])
                    T.add("dve", lambda e, on=on, bcs=bcs, bo=bo: e.tensor_tensor(out=on[0:64, :], in0=self.psum[0:64, bo, :], in1=bcs[0:64, :], op=ALU.mult),
                          r=[self.pb[bo], bcb], w=[onb])
                    T.add("pool", lambda e, on=on, hd=hd, t0=t0: e.dma_start(out=self.oT_s[hd // 2, (hd % 2) * 64:(hd % 2) * 64 + 64, t0:t0 + TT], in_=on[0:64, :]),
                          r=[onb], dma=True)
                return rest

            nxt = prep(0)
            deferred = None
            for u, (it, hh) in enumerate(units):
                tl = it * TT
                t0 = g0 + tl
                hd = 4 * g + hh
                qh, qhb = nxt
                bo = 3 + (st_["obank"] % 2)
                st_["obank"] += 1
                pend = []
                for kc in range(NKC):
                    bs = st_["sbank"] % 3
                    st_["sbank"] += 1
                    T.add("pe", lambda e, bs=bs, hh=hh, kc=kc, qh=qh: e.matmul(self.psum[:, bs, :], lhsT=Kg[:, hh, kc * 128:(kc + 1) * 128], rhs=qh[:, :],
                                                                               start=True, stop=True), r=[qhb, kgm[hh], kgr[hh]], w=[self.pb[bs]])
                    if len(pend) >= 2:
                        pend.pop(0)()
                    if kc == 2 and deferred is not None:
                        deferred()
                        deferred = None
                    if kc == NKC // 2 and u + 1 < len(units):
                        nxt = prep(u + 1)
                    pt, ptb = ptr.next()
                    cross = (g0 == 0) and ((kc * 128) // SEG != tl // SEG)
                    bname = "xbias" if cross else "zero"
                    T.add("act", lambda e, pt=pt, bs=bs, bname=bname: e.activation(out=pt[:, :], in_=self.psum[:, bs, :], func=AF.Exp, scale=scale,
                                                                                bias=self.pcol(bname)), r=[self.pb[bs]], w=[ptb])

                    def pv(pt=pt, ptb=ptb, kc=kc, hh=hh, bo=bo):
                        T.add("pe", lambda e: e.matmul(self.psum[:, bo, :], lhsT=Vg[:, kc, hh, :], rhs=pt[:, :], start=(kc == 0), stop=(kc == NKC - 1)),
                              r=[ptb] + vgs, w=[self.pb[bo]])
                    pend.append(pv)
                for p_ in pend:
                    p_()
                if deferred is not None:
                    deferred()
                deferred = finish_head(bo, hd, t0)
            if deferred is not None:
                deferred()
        T.barrier()

    def mla_m3(self, li):
        T, cfg = self.T, self.cfg
        A = self.arena
        A.reset()
        TT = 512
        wo = A.alloc([128, KC, D], BF16, "wo")
        wob = [Buf() for _ in range(KC)]
        self.load_w(wo, wob, self.mla_wo, KC)
        xr = Ring(A, 2, [128, KC, TT], F32, "x")
        orr = Ring(A, 2, [128, KC, TT], BF16, "o")
        for it in range(cfg.NT // TT):
            t0 = it * TT
            xt, xb = xr.next()
            ot, ob = orr.next()
            T.add("sp", lambda e, xt=xt, t0=t0, src=self.xT: e.dma_start(out=xt[:, :, :], in_=src[:, :, t0:t0 + TT].rearrange("c p t -> p c t")),
                  r=self.xblocks(self.xbuf, t0, TT), w=[xb], dma=True)
            T.add("sp", lambda e, ot=ot, t0=t0: e.dma_start(out=ot[:, :, :], in_=self.oT_s[:, :, t0:t0 + TT].rearrange("c p t -> p c t")), w=[ob], dma=True)
            for oc in range(KC):
                bk = self.bank()

                def my(e, oc=oc, bk=bk, ot=ot):
                    for k in range(KC):
                        ins = e.matmul(self.psum[:, bk, :], lhsT=wo[:, k, oc * 128:(oc + 1) * 128], rhs=ot[:, k, :], start=(k == 0), stop=(k == KC - 1))
                    return ins
                T.add("pe", my, r=[ob] + wob, w=[self.pb[bk]])
                T.add("dve", lambda e, oc=oc, bk=bk, xt=xt: e.tensor_tensor(out=xt[:, oc, :], in0=xt[:, oc, :], in1=self.psum[:, bk, :], op=ALU.add),
                      r=[self.pb[bk], xb], w=[xb])
            T.add("pool", lambda e, xt=xt, t0=t0, dst=self.xT: e.dma_start(out=dst[:, :, t0:t0 + TT].rearrange("c p t -> p c t"), in_=xt[:, :, :]),
                  r=[xb], w=self.xblocks(self.xbuf, t0, TT), dma=True)
        T.barrier()

    def build(self):
        cfg = self.cfg
        self.setup()
        self.phase_in()
        for li in cfg.layers:
            m = li % 3
            if li not in getattr(cfg, "mixers", cfg.layers):
                pass
            elif m == 0 and hasattr(self, "phase_na"):
                self.phase_na(li)
            elif m == 1 and hasattr(self, "phase_lru"):
                self.phase_lru(li)
            elif m == 2 and hasattr(self, "phase_mla"):
                self.phase_mla(li)
            if cfg.do_ffn:
                self.phase_ffn(li)
        self.phase_out()
        nc, T = self.nc, self.T
        from contextlib import ExitStack
        with ExitStack() as es:
            sc = {c: es.enter_context(nc.semaphore(f"s_{c}")) for c in COMPUTE}
            sd = {q: [es.enter_context(nc.semaphore(f"d_{q}{i}")) for i in range(T.n_dma_sems)] for q in ("sp", "pool")}
            T.finalize(sc, sd)
            block = es.enter_context(nc.Block())

            @block.sync
            def _(e):
                T.emit_queue("sp", e)

            @block.tensor
            def _(e):
                T.emit_queue("pe", e)

            @block.scalar
            def _(e):
                T.emit_queue("act", e)

            @block.vector
            def _(e):
                T.emit_queue("dve", e)

            @block.gpsimd
            def _(e):
                T.emit_queue("pool", e)
        return nc


def core_segments(cfg, c):
    nA = cfg.n_cores // 2
    if c < nA:
        return [("p", c, 0), ("p", c, 1), ("s", c, 0)]
    b = c - nA
    return [("s", nA + 3 * b + i, 0) for i in range(3)]


def col_layout(v, ncol):
    return np.ascontiguousarray(np.asarray(v, np.float32).reshape(ncol, 128).T)


def build_pvec(cfg, pv, inp, is_a):
    P = np.zeros((128, pv.n), np.float32)

    def put(name, arr):
        o = pv.off[name]
        P[:, o:o + arr.shape[1]] = arr
    for i in range(4):
        put(f"nmix{i}", col_layout(inp["norm_mix"][i], KC))
        put(f"nffn{i}", col_layout(inp["norm_ffn"][i], KC))
    put("nfin", col_layout(inp["norm_final"], KC))
    cw = np.asarray(inp["lru_conv_w"][0], np.float32)
    put("convw", np.ascontiguousarray(cw.reshape(4, LC, 128).transpose(2, 1, 0).reshape(128, LC * 4)))
    put("convb", col_layout(inp["lru_conv_b"][0], LC))
    for d in range(2):
        put(f"ba{d}", col_layout(inp["lru_b_a"][0, d], LC))
        put(f"bx{d}", col_layout(inp["lru_b_x"][0, d], LC))
        put(f"lam{d}", col_layout(inp["lru_lam"][0, d], LC))
    put("gq", col_layout(inp["mla_g_q"][0], 3))
    put("gkv", col_layout(inp["mla_g_kv"][0], 2))
    put("flag", np.full((128, 1), 1.0 if is_a else 0.0, np.float32))
    put("xbias", np.full((128, 1), 0.0 if is_a else NEG, np.float32))
    put("nab", na_boundary_bias(cfg, is_a))
    put("eps", np.full((128, 1), EPS, np.float32))
    put("one", np.full((128, 1), 1.0, np.float32))
    return P


def na_boundary_bias(cfg, is_a):
    R = cfg.R
    out = np.zeros((128, 4, 6, 2), np.float32)
    pairs = [R - 4, R - 2, R, R + 2]
    starts = [R - 8, R - 8, R - 4, R - 4]
    for pi, (r, k0) in enumerate(zip(pairs, starts)):
        for c in range(6):
            for j in range(2):
                krow = k0 + 2 * c + j
                for jp in range(2):
                    q = r + jp
                    if is_a:
                        rows = 2 * R
                        rs = min(max(q - 4, 0), rows - 8)
                        ok = rs <= krow < rs + 8
                    else:
                        base = 0 if q < R else R
                        rs = min(max(q - base - 4, 0), R - 8) + base
                        ok = rs <= krow < rs + 8
                    out[j * 64:(j + 1) * 64, pi, c, jp] = 0.0 if ok else NEG
    return out.reshape(128, 48)


def na_bias_table(rpb):
    rpb = np.asarray(rpb, np.float32)
    qc = np.arange(GW)
    ws = np.clip(qc - 8, 0, GW - 16)
    kc = np.arange(GW)
    valid = (kc[None, :] >= ws[:, None]) & (kc[None, :] < ws[:, None] + 16)
    dc = np.clip(kc[None, :] - qc[:, None] + 15, 0, 30)
    out = np.full((NH, 2, GW, 16, GW), NEG, np.float32)
    for jp in range(2):
        for e in range(16):
            dr = e - jp
            if 0 <= dr <= 14:
                vals = rpb[:, dr][:, dc]
                out[:, jp, :, e, :] = np.where(valid[None], vals, NEG)
    tc0 = out[:, :, :, 3:5, :].copy()
    tc0[:, 1, :, 0, :] = NEG
    tc4 = out[:, :, :, 11:13, :].copy()
    tc4[:, 0, :, :, :] = NEG
    tc4[:, 1, :, 1, :] = NEG
    blocks = []
    for bi in range(7):
        b = 2 * bi + 1
        blk = out[:, :, :, b:b + 2, :]
        blocks.append(blk.transpose(0, 3, 4, 1, 2).reshape(NH, 128, 128))
    inter = [tc0] + [out[:, :, :, b:b + 2, :] for b in (5, 7, 9)] + [tc4]
    for blk in inter:
        blocks.append(blk.transpose(0, 3, 4, 1, 2).reshape(NH, 128, 128))
    return np.ascontiguousarray(np.concatenate(blocks, axis=2))


def lru_band(w):
    w = np.asarray(w, np.float32)
    dense = np.zeros((LRU_C, LRU_C), np.float32)
    for n in range(16):
        dense[n * 88:(n + 1) * 88, n * 88:(n + 1) * 88] = w[n]
    out = np.zeros((128, LC, 3, 128), np.float32)
    for m in range(LC):
        for kk in range(3):
            k = m + kk - 1
            if 0 <= k < LC:
                out[:, m, kk, :] = dense[k * 128:(k + 1) * 128, m * 128:(m + 1) * 128]
    return out.reshape(128, LC * 3 * 128)


def rope_tables(cfg, is_a):
    SEG = cfg.SEG
    pos = np.concatenate([np.arange(SEG), np.arange(SEG) + (SEG if is_a else 0), np.arange(SEG)]).astype(np.float32)
    inv = (10000.0 ** (-np.arange(0, 32, 2, dtype=np.float32) / 32)).astype(np.float32)
    ang = pos[None, :] * inv[:, None]
    c, s = np.cos(ang).astype(np.float32), np.sin(ang).astype(np.float32)
    C = np.concatenate([c, c], 0)
    S = np.concatenate([-s, s], 0)
    return np.ascontiguousarray(np.stack([C, S], 0))


_SHARED_CACHE = {}


def prepare_inputs(cfg, inp):
    pv = pv_layout()
    inp = {k: np.asarray(v) for k, v in inp.items()}
    f32 = lambda a: np.ascontiguousarray(a, dtype=np.float32)
    sh = {}
    sh["ident"] = np.eye(128, dtype=np.float32)
    sh["ffn_wg"] = f32(inp["ffn_w_gate"])
    sh["ffn_wu"] = f32(inp["ffn_w_up"])
    sh["ffn_wd"] = f32(inp["ffn_w_down"])
    sh["na_wqkv"] = f32(inp["na_w_qkv"])
    sh["na_wo"] = f32(inp["na_w_o"])
    sh["na_btab"] = np.stack([na_bias_table(inp["na_rpb"][j]) for j in range(2)], 0)
    sh["lru_win"] = f32(inp["lru_w_in"][0])
    sh["lru_band"] = np.stack([lru_band(inp["lru_w_a"][0, 0]), lru_band(inp["lru_w_x"][0, 0]),
                               lru_band(inp["lru_w_a"][0, 1]), lru_band(inp["lru_w_x"][0, 1])], 0)
    sh["lru_wout"] = f32(inp["lru_w_out"][0])
    sh["mla_wdq"] = f32(inp["mla_w_dq"][0])
    wdkv = f32(inp["mla_w_dkv"][0])
    sh["mla_wdkv"] = f32(wdkv[:, :256])
    wkr = np.zeros((2, D, 96), np.float32)
    wkr[0, :, 64:96] = wdkv[:, 256:288]
    wkr[1, :, 64:80] = wdkv[:, 272:288]
    wkr[1, :, 80:96] = wdkv[:, 256:272]
    sh["mla_wkr"] = wkr
    wuq = f32(inp["mla_w_uq"][0]).reshape(384, NH, 96)
    wuq_sw = wuq.copy()
    wuq_sw[:, :, 64:80] = wuq[:, :, 80:96]
    wuq_sw[:, :, 80:96] = wuq[:, :, 64:80]
    sh["mla_wuq"] = np.ascontiguousarray(np.stack([wuq.reshape(384, 1536), wuq_sw.reshape(384, 1536)], 0))
    wukv = f32(inp["mla_w_ukv"][0]).reshape(256, NH, 128)
    sh["mla_wuk"] = np.ascontiguousarray(wukv[:, :, :64].reshape(256, 1024))
    sh["mla_wuv"] = np.ascontiguousarray(wukv[:, :, 64:].reshape(256, 1024))
    sh["mla_wo"] = f32(inp["mla_w_o"][0])
    xp, xs = inp["x_prompt"], inp["x_sample"]
    SEG = cfg.SEG
    in_maps = []
    for c in range(cfg.n_cores):
        segs = core_segments(cfg, c)
        is_a = segs[0][0] == "p"
        parts = []
        for (g, i, hlf) in segs:
            if g == "p":
                parts.append(xp[i, hlf * SEG:(hlf + 1) * SEG])
            else:
                parts.append(xs[i])
        m = dict(sh)
        m["x_tok"] = np.ascontiguousarray(np.concatenate(parts, 0), dtype=np.float32)
        m["pvec"] = build_pvec(cfg, pv, inp, is_a)
        m["rope"] = rope_tables(cfg, is_a)
        in_maps.append(m)
    return in_maps


def gather_outputs(cfg, results, n_prompt, n_sample):
    SEG = cfg.SEG
    yp = np.zeros((n_prompt, 2 * SEG, D), np.float32)
    ys = np.zeros((n_sample, SEG, D), np.float32)
    for c in range(cfg.n_cores):
        y = np.asarray(results[c]["y_tok"], np.float32)
        for si, (g, i, hlf) in enumerate(core_segments(cfg, c)):
            blk = y[si * SEG:(si + 1) * SEG]
            if g == "p":
                yp[i, hlf * SEG:(hlf + 1) * SEG] = blk
            else:
                ys[i] = blk
    return yp, ys


_PROG_CACHE = {}


def kernel(**inputs):
    cfg = Cfg()
    in_maps = prepare_inputs(cfg, inputs)
    nc = Prog(cfg).build()
    res = run_bass_kernel_spmd(nc, in_maps, core_ids=list(range(cfg.n_cores)))
    yp, ys = gather_outputs(cfg, res.results, inputs["x_prompt"].shape[0], inputs["x_sample"].shape[0])
    return (yp, ys)
```

```python
import numpy as np
import concourse.bass as bass
import concourse.mybir as mybir
from concourse.bass_utils import run_bass_kernel_spmd

F32 = mybir.dt.float32
BF16 = mybir.dt.bfloat16
AF = mybir.ActivationFunctionType
ALU = mybir.AluOpType

D = 1024
KC = 8
DFF = 2816
FC = 22
GW = 64
NH = 16
LRU_C = 1408
LC = 11
NEG = -30000.0
EPS = 1e-6
GELU_C1 = 0.7978845608028654
GELU_C2 = 0.7978845608028654 * 0.044715


class Cfg:
    def __init__(self, seg_rows=32, n_cores=8, layers=(0, 1, 2, 3), do_ffn=True):
        self.R = seg_rows
        self.SEG = seg_rows * GW
        self.NT = 3 * self.SEG
        self.n_cores = n_cores
        self.layers = tuple(layers)
        self.do_ffn = do_ffn


class Buf:
    __slots__ = ("name", "w", "readers")

    def __init__(self, name=""):
        self.name = name
        self.w = None
        self.readers = []


class Op:
    __slots__ = ("eng", "emit", "waits", "signal", "seq", "is_dma", "sem", "semval")


COMPUTE = ("pe", "act", "dve", "pool")
QUEUES = ("pe", "act", "dve", "pool", "sp")


class Tracker:
    def __init__(self, nc, n_dma_sems=20):
        self.nc = nc
        self.ops = {q: [] for q in QUEUES}
        self.seq = {q: 0 for q in COMPUTE}
        self.waited = {q: {c: 0 for c in COMPUTE} for q in QUEUES}
        self.waited_dma = {q: set() for q in QUEUES}
        self.dma_since_barrier = []
        self.n_dma_sems = n_dma_sems
        self.dma_count = {"sp": 0, "pool": 0}
        self.dma_last_on_sem = {"sp": [None] * n_dma_sems, "pool": [None] * n_dma_sems}
        self.dma_sem_total = {"sp": [0] * n_dma_sems, "pool": [0] * n_dma_sems}

    def _dep(self, op, d, waits):
        q = op.eng
        if d is None:
            return
        if d.is_dma:
            if d in self.waited_dma[q]:
                return
            self.waited_dma[q].add(d)
            waits.append(d)
        else:
            if d.eng == q and q == "pe" and not op.is_dma:
                return
            if self.waited[q][d.eng] >= d.seq:
                return
            self.waited[q][d.eng] = d.seq
            d.signal = True
            waits.append(d)

    def add(self, eng, emit, r=(), w=(), dma=False):
        op = Op()
        op.eng = eng
        op.emit = emit
        op.is_dma = dma
        op.signal = False
        op.sem = None
        op.semval = 0
        waits = []
        for b in r:
            self._dep(op, b.w, waits)
        for b in w:
            self._dep(op, b.w, waits)
            for rd in b.readers:
                self._dep(op, rd, waits)
        if dma:
            j = self.dma_count[eng]
            self.dma_count[eng] += 1
            s = j % self.n_dma_sems
            prev = self.dma_last_on_sem[eng][s]
            if prev is not None:
                self._dep(op, prev, waits)
            self.dma_last_on_sem[eng][s] = op
            self.dma_sem_total[eng][s] += 16
            op.sem = (eng, s)
            op.semval = self.dma_sem_total[eng][s]
            op.seq = -1
            self.dma_since_barrier.append(op)
        else:
            self.seq[eng] += 1
            op.seq = self.seq[eng]
        op.waits = waits
        for b in r:
            if not dma:
                b.readers = [x for x in b.readers if x.is_dma or x.eng != eng]
            b.readers.append(op)
        for b in w:
            b.w = op
            b.readers = []
        self.ops[eng].append(op)
        return op

    def barrier(self):
        last = {}
        for c in COMPUTE:
            for op in reversed(self.ops[c]):
                if not op.is_dma and op.emit is not None:
                    last[c] = op
                    break
        dmas = list(self.dma_since_barrier)
        self.dma_since_barrier = []
        for q in QUEUES:
            op = Op()
            op.eng = q
            op.emit = None
            op.is_dma = False
            op.signal = False
            op.sem = None
            op.semval = 0
            op.seq = self.seq[q] if q in COMPUTE else 0
            waits = []
            fake = Op()
            fake.eng = q
            fake.is_dma = True
            for c, l in last.items():
                if c == q:
                    continue
                self._dep(fake, l, waits)
            for d in dmas:
                self._dep(fake, d, waits)
            op.waits = waits
            self.ops[q].append(op)

    def finalize(self, sems_compute, sems_dma):
        for c in COMPUTE:
            cnt = 0
            for op in self.ops[c]:
                if op.is_dma or op.emit is None:
                    continue
                if op.signal:
                    cnt += 1
                    op.semval = cnt
                    op.sem = sems_compute[c]
        for q in QUEUES:
            for op in self.ops[q]:
                if op.is_dma:
                    op.sem = sems_dma[op.sem[0]][op.sem[1]]

    def emit_queue(self, q, e):
        for op in self.ops[q]:
            for d in op.waits:
                e.wait_ge(d.sem, d.semval)
            if op.emit is None:
                continue
            ins = op.emit(e)
            if op.is_dma:
                ins.then_inc(op.sem, 16)
            elif op.signal:
                ins.then_inc(op.sem, 1)


class Arena:
    def __init__(self, nc, base, limit):
        self.nc = nc
        self.base = base
        self.limit = limit
        self.off = base
        self.uid = 0

    def reset(self):
        self.off = self.base

    def alloc(self, shape, dtype, name="t"):
        esz = 4 if dtype == F32 else 2
        n = 1
        for s in shape[1:]:
            n *= s
        nbytes = (n * esz + 63) // 64 * 64
        off = self.off
        assert off + nbytes <= self.limit, f"SBUF arena overflow: {name} {shape} at {off} (+{nbytes}) limit {self.limit}"
        self.off += nbytes
        self.uid += 1
        return self.nc.alloc_sbuf_tensor_at(f"{name}_{self.uid}", list(shape), dtype, offset=off)


class Ring:
    def __init__(self, arena, n, shape, dtype, name):
        self.tiles = [arena.alloc(shape, dtype, name) for _ in range(n)]
        self.bufs = [Buf(f"{name}{i}") for i in range(n)]
        self.i = 0

    def next(self):
        k = self.i % len(self.tiles)
        self.i += 1
        return self.tiles[k], self.bufs[k]


class PV:
    def __init__(self):
        self.n = 0
        self.off = {}

    def add(self, name, ncol):
        self.off[name] = self.n
        self.n += ncol


def pv_layout():
    pv = PV()
    for i in range(4):
        pv.add(f"nmix{i}", KC)
        pv.add(f"nffn{i}", KC)
    pv.add("nfin", KC)
    pv.add("convw", LC * 4)
    pv.add("convb", LC)
    for d in range(2):
        pv.add(f"ba{d}", LC)
        pv.add(f"bx{d}", LC)
        pv.add(f"lam{d}", LC)
    pv.add("gq", 3)
    pv.add("gkv", 2)
    pv.add("flag", 1)
    pv.add("xbias", 1)
    pv.add("nab", 48)
    pv.add("zero", 1)
    pv.add("eps", 1)
    pv.add("one", 1)
    return pv


class Prog:
    def __init__(self, cfg):
        self.cfg = cfg
        self.nc = bass.Bass("TRN2", target_bir_lowering=False)
        self.T = Tracker(self.nc)
        self.pv = pv_layout()
        nc = self.nc
        NT = cfg.NT
        di = lambda name, shape, dt=F32: nc.dram_tensor(name, list(shape), dt, kind="ExternalInput")
        self.x_tok = di("x_tok", [NT, D])
        self.pvec_d = di("pvec", [128, self.pv.n])
        self.ident_d = di("ident", [128, 128])
        self.ffn_wg = di("ffn_wg", [4, D, DFF])
        self.ffn_wu = di("ffn_wu", [4, D, DFF])
        self.ffn_wd = di("ffn_wd", [4, DFF, D])
        self.na_wqkv = di("na_wqkv", [2, D, 3 * D])
        self.na_wo = di("na_wo", [2, D, D])
        self.na_btab = di("na_btab", [2, NH, 128, 1536])
        self.lru_win = di("lru_win", [D, 2 * LRU_C])
        self.lru_band = di("lru_band", [4, 128, LC * 3 * 128])
        self.lru_wout = di("lru_wout", [LRU_C, D])
        self.mla_wdq = di("mla_wdq", [D, 384])
        self.mla_wdkv = di("mla_wdkv", [D, 256])
        self.mla_wkr = di("mla_wkr", [2, D, 96])
        self.mla_wuq = di("mla_wuq", [2, 384, 1536])
        self.mla_wuk = di("mla_wuk", [256, 1024])
        self.mla_wuv = di("mla_wuv", [256, 1024])
        self.mla_wo = di("mla_wo", [D, D])
        self.rope_d = di("rope", [2, 32, NT])
        self.y_tok = nc.dram_tensor("y_tok", [NT, D], F32, kind="ExternalOutput")
        self.xT = nc.dram_tensor("xT_s", [KC, 128, NT], F32)
        self.xT2 = nc.dram_tensor("xT2_s", [KC, 128, NT], F32)
        self.gg_s = nc.dram_tensor("gg_s", [LC, 128, NT], F32)
        self.xc_s = nc.dram_tensor("xc_s", [LC, 128, NT], F32)
        self.hf_s = nc.dram_tensor("hf_s", [LC, 128, NT], F32)
        self.oT_s = nc.dram_tensor("oT_s", [KC, 128, NT], BF16)
        nblk = NT // 128
        self.xbuf = [Buf(f"x{i}") for i in range(nblk)]
        self.xbuf2 = [Buf(f"xb{i}") for i in range(nblk)]
        self.ggbuf = [Buf(f"gg{i}") for i in range(nblk)]
        self.xcbuf = [Buf(f"xc{i}") for i in range(nblk)]
        self.hfbuf = [Buf(f"hf{i}") for i in range(nblk)]
        self.obuf = [Buf(f"o{i}") for i in range(nblk)]
        self.ybuf = [Buf(f"y{i}") for i in range(nblk)]
        self.persist = Arena(nc, 16512, 16512 + 6 * 1024)
        self.pvec = self.persist.alloc([128, self.pv.n], F32, "pvec")
        self.ident = self.persist.alloc([128, 128], F32, "ident")
        self.identb = self.persist.alloc([128, 128], BF16, "identb")
        self.onesb = self.persist.alloc([128, 128], BF16, "onesb")
        self.onesf = self.persist.alloc([128, 128], F32, "onesf")
        self.lruc = self.persist.alloc([128, 2, 4, LC], F32, "lruc")
        self.pbuf = Buf("persist")
        self.arena = Arena(nc, self.persist.limit, 229376)
        self.psum = nc.alloc_psum_tensor("psum", [128, 8, 512], F32)
        self.pb = [Buf(f"bank{i}") for i in range(8)]
        self.pbi = 0

    def xblocks(self, bufs, t0, n):
        return bufs[t0 // 128:(t0 + n + 127) // 128]

    def pcol(self, name, c=0, n=1):
        o = self.pv.off[name] + c
        return self.pvec[:, o:o + n]

    def bank(self):
        k = self.pbi % 8
        self.pbi += 1
        return k

    def setup(self):
        T = self.T
        T.add("sp", lambda e: e.dma_start(out=self.pvec[:, :], in_=self.pvec_d[:, :]), w=[self.pbuf], dma=True)
        T.add("sp", lambda e: e.dma_start(out=self.ident[:, :], in_=self.ident_d[:, :]), w=[self.pbuf], dma=True)
        T.add("dve", lambda e: e.tensor_copy(out=self.identb[:, :], in_=self.ident[:, :]), r=[self.pbuf], w=[self.pbuf])
        T.add("dve", lambda e: e.memset(self.onesb[:, :], 1.0), w=[self.pbuf])
        T.add("dve", lambda e: e.memset(self.onesf[:, :], 1.0), w=[self.pbuf])
        T.barrier()

    def load_w(self, dst, bufs, src2d, K, c0=None, c1=None):
        for k in range(K):
            if c0 is None:
                src = src2d[k * 128:(k + 1) * 128, :]
            else:
                src = src2d[k * 128:(k + 1) * 128, c0:c1]
            self.T.add("pool", (lambda e, d=dst[:, k, :], s=src: e.dma_start(out=d, in_=s)), w=[bufs[k]], dma=True)

    def rmsnorm(self, xt, xb, KCn, W, gname, h, hb, sq, sqb, rstd, rb, Dn, bk=None):
        T = self.T
        T.add("act", lambda e: e.activation(out=sq[:, 0:KCn, 0:W], in_=xt[:, 0:KCn, 0:W], func=AF.Square), r=[xb], w=[sqb])
        if bk is None:
            bk = self.bank()
        ps = self.psum[:, bk, 0:W]

        def mm(e):
            for c in range(KCn):
                ins = e.matmul(ps, lhsT=self.onesb[:, :], rhs=sq[:, c, 0:W], start=(c == 0), stop=(c == KCn - 1))
            return ins
        T.add("pe", mm, r=[sqb], w=[self.pb[bk]])
        T.add("act", lambda e: e.activation(out=rstd[:, 0:W], in_=ps, func=AF.Ln, scale=1.0 / Dn, bias=self.pcol("eps")),
              r=[self.pb[bk]], w=[rb])
        T.add("act", lambda e: e.activation(out=rstd[:, 0:W], in_=rstd[:, 0:W], func=AF.Exp, scale=-0.5), r=[rb], w=[rb])

        def sc(e):
            for c in range(KCn):
                ins = e.scalar_tensor_tensor(out=h[:, c, 0:W], in0=xt[:, c, 0:W], scalar=self.pcol(gname, c), in1=rstd[:, 0:W],
                                             op0=ALU.mult, op1=ALU.mult)
            return ins
        T.add("dve", sc, r=[xb, rb], w=[hb])

    def phase_in(self):
        T, cfg = self.T, self.cfg
        A = self.arena
        A.reset()
        xin = Ring(A, 3, [128, D], F32, "xin")
        xo = Ring(A, 3, [128, KC, 128], F32, "xo")
        for b in range(cfg.NT // 128):
            t0 = b * 128
            xi, xib = xin.next()
            T.add("sp", lambda e, xi=xi, t0=t0: e.dma_start(out=xi[:, :], in_=self.x_tok[t0:t0 + 128, :]), w=[xib], dma=True)
            b0, b1 = self.bank(), self.bank()
            xot, xob = xo.next()
            for half, bk in ((0, b0), (1, b1)):
                def tr(e, xi=xi, half=half, bk=bk):
                    for j in range(4):
                        c = half * 4 + j
                        ins = e.transpose(self.psum[:, bk, j * 128:(j + 1) * 128], xi[:, c * 128:(c + 1) * 128], self.ident[:, :])
                    return ins
                T.add("pe", tr, r=[xib], w=[self.pb[bk]])
                eng = "act" if half == 0 else "dve"
                if eng == "act":
                    T.add("act", lambda e, xot=xot, bk=bk, half=half: e.activation(
                        out=xot[:, half * 4:half * 4 + 4, :], in_=self.psum[:, bk, :], func=AF.Copy), r=[self.pb[bk]], w=[xob])
                else:
                    T.add("dve", lambda e, xot=xot, bk=bk, half=half: e.tensor_copy(
                        out=xot[:, half * 4:half * 4 + 4, :], in_=self.psum[:, bk, :]), r=[self.pb[bk]], w=[xob])
            T.add("pool", lambda e, xot=xot, t0=t0, dst=self.xT: e.dma_start(
                out=dst[:, :, t0:t0 + 128].rearrange("c p t -> p c t"), in_=xot[:, :, :]),
                r=[xob], w=[self.xbuf[b]], dma=True)
        T.barrier()

    def phase_out(self):
        T, cfg = self.T, self.cfg
        A = self.arena
        A.reset()
        TT = 512
        xr = Ring(A, 2, [128, KC, TT], F32, "fx")
        sqr = Ring(A, 2, [128, KC, TT], BF16, "fsq")
        rs = Ring(A, 2, [128, TT], F32, "frs")
        yo = Ring(A, 3, [128, D], F32, "fy")
        for it in range(cfg.NT // TT):
            t0 = it * TT
            xt, xb = xr.next()
            T.add("sp", lambda e, xt=xt, t0=t0, src=self.xT: e.dma_start(out=xt[:, :, :], in_=src[:, :, t0:t0 + TT].rearrange("c p t -> p c t")),
                  r=self.xblocks(self.xbuf, t0, TT), w=[xb], dma=True)
            sq, sqb = sqr.next()
            rstd, rb = rs.next()
            T.add("act", lambda e, sq=sq, xt=xt: e.activation(out=sq[:, :, :], in_=xt[:, :, :], func=AF.Square), r=[xb], w=[sqb])
            bk = self.bank()
            ps = self.psum[:, bk, 0:TT]

            def mm(e, sq=sq, ps=ps):
                for c in range(KC):
                    ins = e.matmul(ps, lhsT=self.onesb[:, :], rhs=sq[:, c, :], start=(c == 0), stop=(c == KC - 1))
                return ins
            T.add("pe", mm, r=[sqb], w=[self.pb[bk]])
            T.add("act", lambda e, rstd=rstd, ps=ps: e.activation(out=rstd[:, :], in_=ps, func=AF.Ln, scale=1.0 / D, bias=self.pcol("eps")),
                  r=[self.pb[bk]], w=[rb])
            T.add("act", lambda e, rstd=rstd: e.activation(out=rstd[:, :], in_=rstd[:, :], func=AF.Exp, scale=-0.5), r=[rb], w=[rb])

            def sc(e, xt=xt, rstd=rstd):
                for c in range(KC):
                    ins = e.scalar_tensor_tensor(out=xt[:, c, :], in0=xt[:, c, :], scalar=self.pcol("nfin", c), in1=rstd[:, :],
                                                 op0=ALU.mult, op1=ALU.mult)
                return ins
            T.add("dve", sc, r=[xb, rb], w=[xb])
            for s in range(TT // 128):
                yt, yb = yo.next()
                for half in range(2):
                    bk = self.bank()

                    def tr(e, xt=xt, s=s, half=half, bk=bk):
                        for j in range(4):
                            c = half * 4 + j
                            ins = e.transpose(self.psum[:, bk, j * 128:(j + 1) * 128], xt[:, c, s * 128:(s + 1) * 128], self.ident[:, :])
                        return ins
                    T.add("pe", tr, r=[xb], w=[self.pb[bk]])
                    if half == 0:
                        T.add("act", lambda e, yt=yt, bk=bk: e.activation(out=yt[:, 0:512], in_=self.psum[:, bk, :], func=AF.Copy),
                              r=[self.pb[bk]], w=[yb])
                    else:
                        T.add("dve", lambda e, yt=yt, bk=bk: e.tensor_copy(out=yt[:, 512:1024], in_=self.psum[:, bk, :]),
                              r=[self.pb[bk]], w=[yb])
                tt = t0 + s * 128
                T.add("pool", lambda e, yt=yt, tt=tt: e.dma_start(out=self.y_tok[tt:tt + 128, :], in_=yt[:, :]),
                      r=[yb], w=[self.ybuf[tt // 128]], dma=True)
        T.barrier()

    def phase_ffn(self, li):
        T, cfg = self.T, self.cfg
        A = self.arena
        A.reset()
        TT = 256
        wg = A.alloc([128, KC, DFF], BF16, "wg")
        wu = A.alloc([128, KC, DFF], BF16, "wu")
        wd = A.alloc([128, FC, D], BF16, "wd")
        wgb = [Buf() for _ in range(KC)]
        wub = [Buf() for _ in range(KC)]
        wdb = [Buf() for _ in range(FC)]
        self.load_w(wg, wgb, self.ffn_wg[li], KC)
        self.load_w(wu, wub, self.ffn_wu[li], KC)
        self.load_w(wd, wdb, self.ffn_wd[li], FC)
        xr = Ring(A, 2, [128, KC, TT], F32, "x")
        hr = Ring(A, 2, [128, KC, TT], BF16, "h")
        ar = Ring(A, 2, [128, FC, TT], BF16, "a")
        rs = Ring(A, 2, [128, TT], F32, "rs")
        sgr = Ring(A, 4, [128, TT], F32, "sg")
        gname = f"nffn{li}"
        for it in range(cfg.NT // TT):
            t0 = it * TT
            xt, xb = xr.next()
            T.add("sp", lambda e, xt=xt, t0=t0, src=self.xT: e.dma_start(out=xt[:, :, :], in_=src[:, :, t0:t0 + TT].rearrange("c p t -> p c t")),
                  r=self.xblocks(self.xbuf, t0, TT), w=[xb], dma=True)
            h, hb = hr.next()
            a, ab = ar.next()
            rstd, rb = rs.next()
            self.rmsnorm(xt, xb, KC, TT, gname, h, hb, a, ab, rstd, rb, D)
            for j in range(FC):
                bg, bu = self.bank(), self.bank()

                def mmg(e, j=j, bg=bg, h=h):
                    for k in range(KC):
                        ins = e.matmul(self.psum[:, bg, 0:TT], lhsT=wg[:, k, j * 128:(j + 1) * 128], rhs=h[:, k, :],
                                       start=(k == 0), stop=(k == KC - 1))
                    return ins

                def mmu(e, j=j, bu=bu, h=h):
                    for k in range(KC):
                        ins = e.matmul(self.psum[:, bu, 0:TT], lhsT=wu[:, k, j * 128:(j + 1) * 128], rhs=h[:, k, :],
                                       start=(k == 0), stop=(k == KC - 1))
                    return ins
                T.add("pe", mmg, r=[hb] + wgb, w=[self.pb[bg]])
                T.add("pe", mmu, r=[hb] + wub, w=[self.pb[bu]])
                sg, sgb = sgr.next()
                T.add("act", lambda e, sg=sg, bg=bg: e.activation(out=sg[:, :], in_=self.psum[:, bg, 0:TT], func=AF.Tanh, scale=0.5),
                      r=[self.pb[bg]], w=[sgb])
                T.add("dve", lambda e, sg=sg, bg=bg: e.scalar_tensor_tensor(out=sg[:, :], in0=sg[:, :], scalar=1.0, in1=self.psum[:, bg, 0:TT],
                                                                            op0=ALU.add, op1=ALU.mult), r=[sgb, self.pb[bg]], w=[sgb])
                T.add("dve", lambda e, sg=sg, bu=bu, a=a, j=j: e.scalar_tensor_tensor(out=a[:, j, :], in0=sg[:, :], scalar=0.5,
                                                                                      in1=self.psum[:, bu, 0:TT], op0=ALU.mult, op1=ALU.mult),
                      r=[sgb, self.pb[bu]], w=[ab])
            for m in range(KC):
                bk = self.bank()

                def mmd(e, m=m, bk=bk, a=a):
                    for j in range(FC):
                        ins = e.matmul(self.psum[:, bk, 0:TT], lhsT=wd[:, j, m * 128:(m + 1) * 128], rhs=a[:, j, :],
                                       start=(j == 0), stop=(j == FC - 1))
                    return ins
                T.add("pe", mmd, r=[ab] + wdb, w=[self.pb[bk]])
                T.add("dve", lambda e, xt=xt, m=m, bk=bk: e.tensor_tensor(out=xt[:, m, :], in0=xt[:, m, :], in1=self.psum[:, bk, 0:TT], op=ALU.add),
                      r=[self.pb[bk], xb], w=[xb])
            T.add("pool", lambda e, xt=xt, t0=t0, dst=self.xT: e.dma_start(out=dst[:, :, t0:t0 + TT].rearrange("c p t -> p c t"), in_=xt[:, :, :]),
                  r=[xb], w=self.xblocks(self.xbuf, t0, TT), dma=True)
        T.barrier()


    def na_plan(self, group, r):
        R = self.cfg.R
        rows = 2 * R if group == 0 else R
        if group == 0 and r in (R - 4, R - 2, R, R + 2):
            pidx = {R - 4: 0, R - 2: 1, R: 2, R + 2: 3}[r]
            return (R - 8 if r < R else R - 4), 6, "boundary", pidx
        if r < 4:
            return 0, 4, "border", None
        if r >= rows - 4:
            return rows - 8, 4, "border", None
        return r - 4, 5, "interior", None

    def phase_na(self, li):
        T, cfg = self.T, self.cfg
        A = self.arena
        R, SEG = cfg.R, cfg.SEG
        jn = li // 3
        TT = 256
        gname = f"nmix{li}"
        wqkv = self.na_wqkv[jn]
        segs = [(0, 0, 0, R, 0, R + 4), (0, 0, R, 2 * R, R - 4, 2 * R), (1, 2 * SEG, 0, R, 0, R)]
        for sg in segs:
            self.na_segment(li, *sg)
        self.xT, self.xT2 = self.xT2, self.xT
        self.xbuf, self.xbuf2 = self.xbuf2, self.xbuf

    def na_segment(self, li, group, gbase, q0, q1, kr0, kr1):
        T, cfg = self.T, self.cfg
        A = self.arena
        R, SEG = cfg.R, cfg.SEG
        jn = li // 3
        TT = 256
        gname = f"nmix{li}"
        wqkv = self.na_wqkv[jn]
        KLmax = (R + 4) * GW
        if True:
            A.reset()
            KL = (kr1 - kr0) * GW
            KT = A.alloc([128, KC, KLmax], BF16, "KT")
            V = A.alloc([128, KLmax // 128, D], BF16, "V")
            mark = A.off
            wk = A.alloc([128, KC, D], BF16, "wk")
            wv = A.alloc([128, KC, D], BF16, "wv")
            wkb = [Buf() for _ in range(KC)]
            wvb = [Buf() for _ in range(KC)]
            self.load_w(wk, wkb, wqkv, KC, D, 2 * D)
            self.load_w(wv, wvb, wqkv, KC, 2 * D, 3 * D)
            xr = Ring(A, 2, [128, KC, TT], F32, "x")
            hr = Ring(A, 2, [128, KC, TT], BF16, "h")
            sqr = Ring(A, 2, [128, KC, TT], BF16, "sq")
            rs = Ring(A, 2, [128, TT], F32, "rs")
            kvb = Buf("kv")
            kvb2 = Buf("kv2")
            for it in range(KL // TT):
                tl = it * TT
                t0 = gbase + kr0 * GW + tl
                xt, xb = xr.next()
                T.add("sp", lambda e, xt=xt, t0=t0, src=self.xT: e.dma_start(out=xt[:, :, :], in_=src[:, :, t0:t0 + TT].rearrange("c p t -> p c t")),
                      r=self.xblocks(self.xbuf, t0, TT), w=[xb], dma=True)
                h, hb = hr.next()
                sq, sqb = sqr.next()
                rstd, rb = rs.next()
                self.rmsnorm(xt, xb, KC, TT, gname, h, hb, sq, sqb, rstd, rb, D)
                for oc in range(KC):
                    bk = self.bank()

                    def mmk(e, oc=oc, bk=bk, h=h):
                        for k in range(KC):
                            ins = e.matmul(self.psum[:, bk, 0:TT], lhsT=wk[:, k, oc * 128:(oc + 1) * 128], rhs=h[:, k, :],
                                           start=(k == 0), stop=(k == KC - 1))
                        return ins
                    T.add("pe", mmk, r=[hb] + wkb, w=[self.pb[bk]])
                    if oc % 2 == 0:
                        T.add("act", lambda e, oc=oc, bk=bk, tl=tl: e.activation(out=KT[:, oc, tl:tl + TT], in_=self.psum[:, bk, 0:TT], func=AF.Copy),
                              r=[self.pb[bk]], w=[kvb])
                    else:
                        T.add("dve", lambda e, oc=oc, bk=bk, tl=tl: e.tensor_copy(out=KT[:, oc, tl:tl + TT], in_=self.psum[:, bk, 0:TT]),
                              r=[self.pb[bk]], w=[kvb2])
                for st in range(TT // 128):
                    for half in range(2):
                        bk = self.bank()

                        def mmv(e, st=st, half=half, bk=bk, h=h):
                            for k in range(KC):
                                ins = e.matmul(self.psum[:, bk, :], lhsT=h[:, k, st * 128:(st + 1) * 128], rhs=wv[:, k, half * 512:(half + 1) * 512],
                                               start=(k == 0), stop=(k == KC - 1))
                            return ins
                        T.add("pe", mmv, r=[hb] + wvb, w=[self.pb[bk]])
                        vt = (tl + st * 128) // 128
                        if half == 0:
                            T.add("act", lambda e, vt=vt, bk=bk: e.activation(out=V[:, vt, 0:512], in_=self.psum[:, bk, :], func=AF.Copy),
                                  r=[self.pb[bk]], w=[kvb])
                        else:
                            T.add("dve", lambda e, vt=vt, bk=bk: e.tensor_copy(out=V[:, vt, 512:1024], in_=self.psum[:, bk, :]),
                                  r=[self.pb[bk]], w=[kvb2])
            T.barrier()
            A.off = mark
            wq = A.alloc([128, KC, D], BF16, "wq")
            wo = A.alloc([128, KC, D], BF16, "wo")
            bt = A.alloc([128, NH, 1536], BF16, "bt")
            wqb = [Buf() for _ in range(KC)]
            wob = [Buf() for _ in range(KC)]
            btb = [Buf() for _ in range(NH)]
            self.load_w(wq, wqb, wqkv, KC, 0, D)
            for hh in range(NH):
                T.add("pool", lambda e, hh=hh: e.dma_start(out=bt[:, hh, :], in_=self.na_btab[jn][hh]), w=[btb[hh]], dma=True)
            self.load_w(wo, wob, self.na_wo[jn], KC)
            xr = Ring(A, 2, [128, KC, TT], F32, "x")
            hr = Ring(A, 1, [128, KC, TT], BF16, "h")
            rs = Ring(A, 1, [128, TT], F32, "rs")
            qr = Ring(A, 2, [128, KC, TT], BF16, "q")
            orr = Ring(A, 2, [128, KC, TT], BF16, "o")
            ptr = Ring(A, 2, [128, 2, 768], BF16, "pt")
            ptb2 = [Buf("ptb_o0"), Buf("ptb_o1")]
            psmb2 = [Buf("psm_o0"), Buf("psm_o1")]
            rdr = Ring(A, 2, [128, 256], F32, "rd")
            ucount = 0
            sumr = Ring(A, 2, [128, 2, 128], BF16, "psm")
            ntile = (q1 - q0) * GW // TT

            def prologue(it):
                r0 = q0 + it * (TT // GW)
                t0 = gbase + r0 * GW
                xt, xb = xr.next()
                T.add("sp", lambda e, xt=xt, t0=t0, src=self.xT: e.dma_start(out=xt[:, :, :], in_=src[:, :, t0:t0 + TT].rearrange("c p t -> p c t")),
                      r=self.xblocks(self.xbuf, t0, TT), w=[xb], dma=True)
                h, hb = hr.next()
                rstd, rb = rs.next()
                self.rmsnorm(xt, xb, KC, TT, gname, h, hb, h, hb, rstd, rb, D, bk=6)
                qt, qb = qr.next()
                for oc in range(KC):
                    bk = 6 + (oc % 2)

                    def mmq(e, oc=oc, bk=bk, h=h):
                        for k in range(KC):
                            ins = e.matmul(self.psum[:, bk, 0:TT], lhsT=wq[:, k, oc * 128:(oc + 1) * 128], rhs=h[:, k, :],
                                           start=(k == 0), stop=(k == KC - 1))
                        return ins
                    T.add("pe", mmq, r=[hb] + wqb, w=[self.pb[bk]])
                    T.add("act", lambda e, oc=oc, bk=bk, qt=qt: e.activation(out=qt[:, oc, :], in_=self.psum[:, bk, 0:TT], func=AF.Identity, scale=0.125),
                          r=[self.pb[bk]], w=[qb])
                return (r0, t0, xt, xb, qt, qb)

            cur = prologue(0)
            for it in range(ntile):
                r0, t0, xt, xb, qt, qb = cur
                ot, ob = orr.next()
                units = []
                for p in range(TT // (2 * GW)):
                    r = r0 + 2 * p
                    k0, nch, kind, pidx = self.na_plan(group, r)
                    for hp in range(NH // 2):
                        units.append((p, r, k0, nch, kind, pidx, hp))
                pend = None
                pend_dve = None
                for ui, (p, r, k0, nch, kind, pidx, hp) in enumerate(units):
                    if ui == len(units) // 2 and it + 1 < ntile:
                        cur = prologue(it + 1)
                    qs = p * 128
                    pt, ptb = ptr.next()
                    ptbs = (ptb, ptb2[(ptr.i - 1) % 2])
                    psm, psmb = sumr.next()
                    psmbs = (psmb, psmb2[(sumr.i - 1) % 2])
                    bo = 4 + (ui % 2)
                    psO = self.psum[:, bo, :]
                    for par in range(2):
                        hd = 2 * hp + par
                        hs = par * 64
                        sb0 = (ucount % 2) * 2
                        ucount += 1
                        psS = self.psum[:, sb0:sb0 + 2, :].rearrange("p b n -> p (b n)")

                        def mms(e, psS=psS, k0=k0, nch=nch, r=r, hp=hp, hs=hs, hd=hd, qt=qt, qs=qs, kind=kind):
                            if kind == "interior":
                                c0 = 7 * 128
                            else:
                                c0 = ((k0 - r + 7 - 1) // 2) * 128
                            n1 = min(nch, 4) * 128
                            e.matmul(psS[:, 0:n1], lhsT=self.identb[:, :], rhs=bt[:, hd, c0:c0 + n1], start=True, stop=False)
                            for c in range(min(nch, 4)):
                                ktok = (k0 + 2 * c - kr0) * GW
                                ins = e.matmul(psS[:, c * 128:(c + 1) * 128], lhsT=KT[hs:hs + 64, hp, ktok:ktok + 128], rhs=qt[hs:hs + 64, hp, qs:qs + 128],
                                               start=False, stop=(c == min(nch, 4) - 1))
                            if nch > 4:
                                e.matmul(psS[:, 512:nch * 128], lhsT=self.identb[:, :], rhs=bt[:, hd, c0 + 512:c0 + nch * 128], start=True, stop=False)
                                for c in range(4, nch):
                                    ktok = (k0 + 2 * c - kr0) * GW
                                    ins = e.matmul(psS[:, c * 128:(c + 1) * 128], lhsT=KT[hs:hs + 64, hp, ktok:ktok + 128], rhs=qt[hs:hs + 64, hp, qs:qs + 128],
                                                   start=False, stop=(c == nch - 1))
                            return ins
                        T.add("pe", mms, r=[qb, btb[hd]], w=[self.pb[sb0], self.pb[sb0 + 1]])
                        if par == 0 and pend is not None:
                            pend_dve = pend()
                            pend = None
                        if kind != "boundary":
                            def ex0(e, pt=pt, psS=psS, nch=nch, par=par):
                                ins = e.activation(out=pt[:, par, 0:512], in_=psS[:, 0:512], func=AF.Exp)
                                if nch > 4:
                                    ins = e.activation(out=pt[:, par, 512:nch * 128], in_=psS[:, 512:nch * 128], func=AF.Exp)
                                return ins
                            T.add("act", ex0, r=[self.pb[sb0], self.pb[sb0 + 1]], w=[ptbs[par]])
                        else:
                            def ex(e, pt=pt, psS=psS, pidx=pidx, par=par):
                                for c in range(6):
                                    for jp in range(2):
                                        o = c * 128 + jp * 64
                                        ins = e.activation(out=pt[:, par, o:o + 64], in_=psS[:, o:o + 64], func=AF.Exp,
                                                           bias=self.pcol("nab", (pidx * 6 + c) * 2 + jp))
                                return ins
                            T.add("act", ex, r=[self.pb[sb0], self.pb[sb0 + 1]], w=[ptbs[par]])


                    def finish(pt=pt, ptbs=ptbs, psm=psm, psmbs=psmbs, psO=psO, bo=bo, k0=k0, nch=nch, hp=hp, ot=ot, ob=ob, qs=qs):
                        def mmo(e):
                            for c in range(nch):
                                vt = (k0 + 2 * c - kr0) // 2
                                ins = e.matmul(psO[:, 0:256], lhsT=V[:, vt, hp * 128:(hp + 1) * 128], rhs=pt[:, :, c * 128:(c + 1) * 128],
                                               start=(c == 0), stop=(c == nch - 1))
                            return ins
                        T.add("pe", mmo, r=list(ptbs), w=[self.pb[bo]])

                        def mmd(e):
                            for c in range(nch):
                                ins = e.matmul(psO[:, 256:512], lhsT=self.onesb[:, :], rhs=pt[:, :, c * 128:(c + 1) * 128],
                                               start=(c == 0), stop=(c == nch - 1))
                            return ins
                        T.add("pe", mmd, r=list(ptbs), w=[self.pb[bo]])

                        def finish_dve():
                            rd, rdb = rdr.next()
                            T.add("dve", lambda e: e.reciprocal(out=rd[:, :], in_=psO[:, 256:512]), r=[self.pb[bo]], w=[rdb])

                            def nrm(e):
                                e.tensor_tensor(out=ot[0:64, hp, qs:qs + 128], in0=psO[0:64, 0:128], in1=rd[0:64, 0:128], op=ALU.mult)
                                return e.tensor_tensor(out=ot[64:128, hp, qs:qs + 128], in0=psO[64:128, 128:256], in1=rd[64:128, 128:256], op=ALU.mult)
                            T.add("dve", nrm, r=[self.pb[bo], rdb], w=[ob])
                        return finish_dve
                    if pend_dve is not None:
                        pend_dve()
                        pend_dve = None
                    pend = finish
                if pend is not None:
                    pend_dve = pend()
                    pend = None
                if pend_dve is not None:
                    pend_dve()
                    pend_dve = None
                for oc in range(KC):
                    bk = 6 + (oc % 2)

                    def mmy(e, oc=oc, bk=bk, ot=ot):
                        for k in range(KC):
                            ins = e.matmul(self.psum[:, bk, 0:TT], lhsT=wo[:, k, oc * 128:(oc + 1) * 128], rhs=ot[:, k, :],
                                           start=(k == 0), stop=(k == KC - 1))
                        return ins
                    T.add("pe", mmy, r=[ob] + wob, w=[self.pb[bk]])
                    T.add("dve", lambda e, oc=oc, bk=bk, xt=xt: e.tensor_tensor(out=xt[:, oc, :], in0=xt[:, oc, :], in1=self.psum[:, bk, 0:TT], op=ALU.add),
                          r=[self.pb[bk], xb], w=[xb])
                T.add("pool", lambda e, xt=xt, t0=t0, dst=self.xT2: e.dma_start(out=dst[:, :, t0:t0 + TT].rearrange("c p t -> p c t"), in_=xt[:, :, :]),
                      r=[xb], w=self.xblocks(self.xbuf2, t0, TT), dma=True)
            T.barrier()


    def lru_prep(self):
        T = self.T
        lc = self.lruc
        for d in range(2):
            lam = self.pcol(f"lam{d}", 0, LC)
            T.add("act", lambda e, d=d, lam=lam: e.activation(out=lc[:, d, 0, :], in_=lam, func=AF.Exp, scale=-1.0), r=[self.pbuf], w=[self.pbuf])
            T.add("act", lambda e, d=d: e.activation(out=lc[:, d, 0, :], in_=lc[:, d, 0, :], func=AF.Ln, bias=self.pcol("one")), r=[self.pbuf], w=[self.pbuf])
            T.add("dve", lambda e, d=d: e.tensor_scalar(out=lc[:, d, 1, :], in0=lc[:, d, 0, :], scalar1=-4.0, scalar2=None, op0=ALU.mult), r=[self.pbuf], w=[self.pbuf])
            T.add("dve", lambda e, d=d: e.tensor_scalar(out=lc[:, d, 0, :], in0=lc[:, d, 0, :], scalar1=-8.0, scalar2=None, op0=ALU.mult), r=[self.pbuf], w=[self.pbuf])
            T.add("dve", lambda e, d=d: e.tensor_scalar(out=lc[:, d, 2, :], in0=self.pcol(f"ba{d}", 0, LC), scalar1=0.5, scalar2=None, op0=ALU.mult), r=[self.pbuf], w=[self.pbuf])
            T.add("dve", lambda e, d=d: e.tensor_scalar(out=lc[:, d, 3, :], in0=self.pcol(f"bx{d}", 0, LC), scalar1=0.5, scalar2=None, op0=ALU.mult), r=[self.pbuf], w=[self.pbuf])
        T.barrier()

    def phase_lru(self, li):
        self.lru_prep()
        self.lru_l1(li)
        self.lru_l23(li, 0)
        self.lru_l23(li, 1)

    def lru_l1(self, li):
        T, cfg = self.T, self.cfg
        A = self.arena
        A.reset()
        SEG, NT = cfg.SEG, cfg.NT
        TT, W = 256, 259
        gname = f"nmix{li}"
        win = A.alloc([128, KC, 2 * LRU_C], BF16, "win")
        winb = [Buf() for _ in range(KC)]
        self.load_w(win, winb, self.lru_win, KC)
        xr_ = Ring(A, 2, [128, KC, W], F32, "x")
        hr = Ring(A, 2, [128, KC, W], BF16, "h")
        sqr = Ring(A, 1, [128, KC, W], BF16, "sq")
        rs = Ring(A, 2, [128, W], F32, "rs")
        xrr = Ring(A, 2, [128, LC, W], F32, "xr")
        xcr = Ring(A, 2, [128, LC, TT], F32, "xc")
        ggr = Ring(A, 2, [128, LC, TT], F32, "gg")
        tmr = Ring(A, 3, [128, TT], F32, "tm")
        groups = [(0, 2 * SEG), (2 * SEG, 3 * SEG)]

        def prologue(it):
            t0 = it * TT
            g0, g1 = groups[0] if t0 < 2 * SEG else groups[1]
            lo, hi = max(t0 - 2, g0), min(t0 + TT + 1, g1)
            xt, xb = xr_.next()
            if lo > t0 - 2:
                T.add("dve", lambda e, xt=xt: e.memset(xt[:, :, 0:2], 0.0), w=[xb])
            if hi < t0 + TT + 1:
                T.add("dve", lambda e, xt=xt: e.memset(xt[:, :, W - 1:W], 0.0), w=[xb])
            c0 = lo - (t0 - 2)
            T.add("sp", lambda e, xt=xt, lo=lo, hi=hi, c0=c0, src=self.xT: e.dma_start(
                out=xt[:, :, c0:c0 + hi - lo], in_=src[:, :, lo:hi].rearrange("c p t -> p c t")),
                r=self.xblocks(self.xbuf, lo, hi - lo), w=[xb], dma=True)
            if t0 == SEG:
                T.add("dve", lambda e, xt=xt: e.tensor_scalar(out=xt[:, :, 0:2], in0=xt[:, :, 0:2], scalar1=self.pcol("flag"), scalar2=None, op0=ALU.mult),
                      r=[xb], w=[xb])
            if t0 + TT == SEG:
                T.add("dve", lambda e, xt=xt: e.tensor_scalar(out=xt[:, :, W - 1:W], in0=xt[:, :, W - 1:W], scalar1=self.pcol("flag"), scalar2=None, op0=ALU.mult),
                      r=[xb], w=[xb])
            h, hb = hr.next()
            sq, sqb = sqr.next()
            rstd, rb = rs.next()
            self.rmsnorm(xt, xb, KC, W, gname, h, hb, sq, sqb, rstd, rb, D)
            return h, hb

        nxt = prologue(0)
        for it in range(NT // TT):
            t0 = it * TT
            h, hb = nxt
            gg, ggb = ggr.next()
            xr, xrb = xrr.next()
            xc, xcb_ = xcr.next()
            pendB = None
            for j in range(2 * LC):
                bk = self.bank()
                isg = j < LC
                c_lo, c_n = (2, TT) if isg else (0, W)

                def mm(e, j=j, bk=bk, h=h, c_lo=c_lo, c_n=c_n):
                    for k in range(KC):
                        ins = e.matmul(self.psum[:, bk, 0:c_n], lhsT=win[:, k, j * 128:(j + 1) * 128], rhs=h[:, k, c_lo:c_lo + c_n],
                                       start=(k == 0), stop=(k == KC - 1))
                    return ins
                T.add("pe", mm, r=[hb] + winb, w=[self.pb[bk]])
                ps = self.psum[:, bk, 0:TT]
                if isg:
                    tm, tmb = tmr.next()
                    T.add("act", lambda e, tm=tm, ps=ps: e.activation(out=tm[:, :], in_=ps, func=AF.Square, scale=float(GELU_C2 ** 0.5)), r=[self.pb[bk]], w=[tmb])
                    T.add("dve", lambda e, tm=tm, ps=ps: e.scalar_tensor_tensor(out=tm[:, :], in0=tm[:, :], scalar=GELU_C1, in1=ps, op0=ALU.add, op1=ALU.mult),
                          r=[tmb, self.pb[bk]], w=[tmb])
                    if pendB is not None:
                        pendB()

                    def stageB(tm=tm, tmb=tmb, ps=ps, bk=bk, j=j, gg=gg, ggb=ggb):
                        T.add("act", lambda e: e.activation(out=tm[:, :], in_=tm[:, :], func=AF.Tanh), r=[tmb], w=[tmb])
                        T.add("dve", lambda e: e.scalar_tensor_tensor(out=gg[:, j, :], in0=tm[:, :], scalar=1.0, in1=ps, op0=ALU.add, op1=ALU.mult),
                              r=[tmb, self.pb[bk]], w=[ggb])
                    pendB = stageB
                    if j == LC - 1:
                        pendB()
                        pendB = None
                        if it + 1 < NT // TT:
                            nxt = prologue(it + 1)
                else:
                    T.add("act", lambda e, xr=xr, j=j, bk=bk: e.activation(out=xr[:, j - LC, :], in_=self.psum[:, bk, 0:W], func=AF.Copy),
                          r=[self.pb[bk]], w=[xrb])
            def conv0(e, xr=xr, xc=xc):
                for c in range(LC):
                    ins = e.activation(out=xc[:, c, :], in_=xr[:, c, 0:TT], func=AF.Identity, scale=self.pcol("convw", c * 4), bias=self.pcol("convb", c))
                return ins
            T.add("act", conv0, r=[xrb], w=[xcb_])
            for k in range(1, 4):
                def convk(e, xr=xr, xc=xc, k=k):
                    for c in range(LC):
                        ins = e.scalar_tensor_tensor(out=xc[:, c, :], in0=xr[:, c, k:k + TT], scalar=self.pcol("convw", c * 4 + k), in1=xc[:, c, :],
                                                     op0=ALU.mult, op1=ALU.add)
                    return ins
                T.add("dve", convk, r=[xrb, xcb_], w=[xcb_])
            T.add("pool", lambda e, xc=xc, t0=t0: e.dma_start(out=self.xc_s[:, :, t0:t0 + TT].rearrange("c p t -> p c t"), in_=xc[:, :, :]),
                  r=[xcb_], w=self.xblocks(self.xcbuf, t0, TT), dma=True)
            T.add("pool", lambda e, gg=gg, t0=t0: e.dma_start(out=self.gg_s[:, :, t0:t0 + TT].rearrange("c p t -> p c t"), in_=gg[:, :, :]),
                  r=[ggb], w=self.xblocks(self.ggbuf, t0, TT), dma=True)
        T.barrier()

    def lru_l23(self, li, d):
        T, cfg = self.T, self.cfg
        A = self.arena
        A.reset()
        SEG, NT = cfg.SEG, cfg.NT
        TT = 256
        lc = self.lruc
        band = A.alloc([128, 2, LC * 3 * 128], BF16, "band")
        bandb = [Buf(), Buf()]
        for i in range(2):
            T.add("pool", lambda e, i=i: e.dma_start(out=band[:, i, :], in_=self.lru_band[2 * d + i]), w=[bandb[i]], dma=True)
        if d == 1:
            wout = A.alloc([128, LC, D], BF16, "wout")
            woutb = [Buf() for _ in range(LC)]
            self.load_w(wout, woutb, self.lru_wout, LC)
            hfr = Ring(A, 1, [128, LC, TT], F32, "hf")
            ggr = Ring(A, 1, [128, LC, TT], F32, "gg")
            xr_ = Ring(A, 1, [128, KC, TT], F32, "x")
            ybr = Ring(A, 1, [128, LC, TT], BF16, "yb")
        xcr = Ring(A, 2, [128, LC, TT], F32, "xc")
        xbr = Ring(A, 2, [128, LC, TT], BF16, "xcb")
        cfull = A.alloc([128, LC, TT], F32, "cfull")
        cfb = Buf("cfull")
        T.add("dve", lambda e: e.memset(cfull[:, :, :], 1.0), w=[cfb])

        def cfill(e):
            for m in range(LC):
                ins = e.tensor_scalar(out=cfull[:, m, :], in0=cfull[:, m, :], scalar1=lc[:, d, 0, m:m + 1], scalar2=None, op0=ALU.mult)
            return ins
        T.add("dve", cfill, r=[cfb, self.pbuf], w=[cfb])
        ar = Ring(A, 2, [128, LC, TT], F32, "a")
        mr = Ring(A, 2, [128, LC, TT], F32, "m")
        tir = Ring(A, 2, [128, LC, TT], F32, "ti")
        hr = Ring(A, 2 if d == 0 else 1, [128, LC, TT], F32, "hh")
        st = A.alloc([128, LC], F32, "st")
        stb = Buf("st")
        ntile = NT // TT
        order = list(range(ntile)) if d == 0 else list(range(ntile - 1, -1, -1))
        groups = [(0, 2 * SEG), (2 * SEG, 3 * SEG)]
        pre_cache = {}

        def pre(it2):
            t2 = it2 * TT
            xc2, xcb2 = xcr.next()
            T.add("sp", lambda e: e.dma_start(out=xc2[:, :, :], in_=self.xc_s[:, :, t2:t2 + TT].rearrange("c p t -> p c t")),
                  r=self.xblocks(self.xcbuf, t2, TT), w=[xcb2], dma=True)
            xcbf2, xcbfb2 = xbr.next()
            T.add("dve", lambda e: e.tensor_copy(out=xcbf2[:, :, :], in_=xc2[:, :, :]), r=[xcb2], w=[xcbfb2])
            pre_cache[it2] = (xc2, xcb2, xcbf2, xcbfb2)

        for oi, it in enumerate(order):
            t0 = it * TT
            g0, g1 = groups[0] if t0 < 2 * SEG else groups[1]
            first = (t0 == g0) if d == 0 else (t0 + TT == g1)
            if first:
                T.add("dve", lambda e: e.memset(st[:, :], 0.0), w=[stb])
            if it not in pre_cache:
                pre(it)
            xc, xcb_, xcbf, xcbfb = pre_cache.pop(it)
            if d == 1:
                hf, hfb = hfr.next()
                gg, ggb = ggr.next()
                xt, xb = xr_.next()
                T.add("sp", lambda e, hf=hf, t0=t0: e.dma_start(out=hf[:, :, :], in_=self.hf_s[:, :, t0:t0 + TT].rearrange("c p t -> p c t")),
                      r=self.xblocks(self.hfbuf, t0, TT), w=[hfb], dma=True)
                T.add("sp", lambda e, gg=gg, t0=t0: e.dma_start(out=gg[:, :, :], in_=self.gg_s[:, :, t0:t0 + TT].rearrange("c p t -> p c t")),
                      r=self.xblocks(self.ggbuf, t0, TT), w=[ggb], dma=True)
                T.add("sp", lambda e, xt=xt, t0=t0, src=self.xT: e.dma_start(out=xt[:, :, :], in_=src[:, :, t0:t0 + TT].rearrange("c p t -> p c t")),
                      r=self.xblocks(self.xbuf, t0, TT), w=[xb], dma=True)
            a, ab = ar.next()
            mm_, mb = mr.next()
            ti, tib = tir.next()
            hh, hhb = hr.next()
            for m in range(LC):
                ba_, bx_ = self.bank(), self.bank()
                kks = [kk for kk in range(3) if 0 <= m + kk - 1 < LC]

                def mg(e, m=m, ba_=ba_, bx_=bx_, kks=kks, xcbf=xcbf):
                    for i, bk in ((0, ba_), (1, bx_)):
                        for n, kk in enumerate(kks):
                            o = (m * 3 + kk) * 128
                            ins = e.matmul(self.psum[:, bk, 0:TT], lhsT=band[:, i, o:o + 128], rhs=xcbf[:, m + kk - 1, :],
                                           start=(n == 0), stop=(n == len(kks) - 1))
                    return ins
                T.add("pe", mg, r=[xcbfb] + bandb, w=[self.pb[ba_], self.pb[bx_]])
                T.add("act", lambda e, a=a, ba_=ba_, m=m: e.activation(out=a[:, m, :], in_=self.psum[:, ba_, 0:TT], func=AF.Tanh, scale=0.5,
                                                                        bias=lc[:, d, 2, m:m + 1]), r=[self.pb[ba_], self.pbuf], w=[ab])
                T.add("act", lambda e, ti=ti, bx_=bx_, m=m: e.activation(out=ti[:, m, :], in_=self.psum[:, bx_, 0:TT], func=AF.Tanh, scale=0.5,
                                                                          bias=lc[:, d, 3, m:m + 1]), r=[self.pb[bx_]], w=[tib])
            if oi + 1 < len(order):
                pre(order[oi + 1])
            T.add("dve", lambda e, a=a: e.scalar_tensor_tensor(out=a[:, :, :], in0=a[:, :, :], scalar=1.0, in1=cfull[:, :, :], op0=ALU.add, op1=ALU.mult),
                  r=[ab, cfb], w=[ab])
            T.add("act", lambda e, a=a, mm_=mm_: e.activation(out=mm_[:, :, :], in_=a[:, :, :], func=AF.Exp), r=[ab], w=[mb])
            T.add("act", lambda e, a=a: e.activation(out=a[:, :, :], in_=a[:, :, :], func=AF.Exp, scale=0.5), r=[ab], w=[ab])
            T.add("act", lambda e, mm_=mm_: e.activation(out=mm_[:, :, :], in_=mm_[:, :, :], func=AF.Sqrt, scale=-1.0, bias=self.pcol("one")), r=[mb], w=[mb])
            T.add("dve", lambda e, ti=ti, xc=xc: e.scalar_tensor_tensor(out=ti[:, :, :], in0=ti[:, :, :], scalar=1.0, in1=xc[:, :, :],
                                                                        op0=ALU.add, op1=ALU.mult), r=[tib, xcb_], w=[tib])
            T.add("dve", lambda e, ti=ti, mm_=mm_: e.scalar_tensor_tensor(out=ti[:, :, :], in0=ti[:, :, :], scalar=0.5, in1=mm_[:, :, :],
                                                                          op0=ALU.mult, op1=ALU.mult), r=[tib, mb], w=[tib])

            def scan(e, a=a, ti=ti, hh=hh):
                for m in range(LC):
                    if d == 0:
                        ins = e.tensor_tensor_scan(out=hh[:, m, :], data0=a[:, m, :], data1=ti[:, m, :], initial=st[:, m:m + 1],
                                                   op0=ALU.mult, op1=ALU.add)
                    else:
                        ins = e.tensor_tensor_scan(out=hh[:, m, ::-1], data0=a[:, m, ::-1], data1=ti[:, m, ::-1], initial=st[:, m:m + 1],
                                                   op0=ALU.mult, op1=ALU.add)
                return ins
            T.add("dve", scan, r=[ab, tib, stb], w=[hhb])
            col = TT - 1 if d == 0 else 0
            crossing = (t0 + TT == SEG) if d == 0 else (t0 == SEG)
            if crossing:
                T.add("dve", lambda e, hh=hh, col=col: e.tensor_scalar(out=st[:, :], in0=hh[:, :, col], scalar1=self.pcol("flag"), scalar2=None, op0=ALU.mult),
                      r=[hhb], w=[stb])
            else:
                T.add("dve", lambda e, hh=hh, col=col: e.tensor_copy(out=st[:, :], in_=hh[:, :, col]), r=[hhb], w=[stb])
            if d == 0:
                T.add("pool", lambda e, hh=hh, t0=t0: e.dma_start(out=self.hf_s[:, :, t0:t0 + TT].rearrange("c p t -> p c t"), in_=hh[:, :, :]),
                      r=[hhb], w=self.xblocks(self.hfbuf, t0, TT), dma=True)
                continue
            yb, ybb = ybr.next()
            T.add("dve", lambda e, hf=hf, hh=hh: e.tensor_tensor(out=hf[:, :, :], in0=hf[:, :, :], in1=hh[:, :, :], op=ALU.add), r=[hfb, hhb], w=[hfb])
            T.add("dve", lambda e, hf=hf, gg=gg, yb=yb: e.scalar_tensor_tensor(out=yb[:, :, :], in0=hf[:, :, :], scalar=0.5, in1=gg[:, :, :],
                                                                                 op0=ALU.mult, op1=ALU.mult), r=[hfb, ggb], w=[ybb])
            for oc in range(KC):
                bk = self.bank()

                def mo(e, oc=oc, bk=bk, yb=yb):
                    for k in range(LC):
                        ins = e.matmul(self.psum[:, bk, 0:TT], lhsT=wout[:, k, oc * 128:(oc + 1) * 128], rhs=yb[:, k, :],
                                       start=(k == 0), stop=(k == LC - 1))
                    return ins
                T.add("pe", mo, r=[ybb] + woutb, w=[self.pb[bk]])
                T.add("dve", lambda e, oc=oc, bk=bk, xt=xt: e.tensor_tensor(out=xt[:, oc, :], in0=xt[:, oc, :], in1=self.psum[:, bk, 0:TT], op=ALU.add),
                      r=[self.pb[bk], xb], w=[xb])
            T.add("pool", lambda e, xt=xt, t0=t0, dst=self.xT: e.dma_start(out=dst[:, :, t0:t0 + TT].rearrange("c p t -> p c t"), in_=xt[:, :, :]),
                  r=[xb], w=self.xblocks(self.xbuf, t0, TT), dma=True)
        T.barrier()


    def phase_mla(self, li):
        cfg = self.cfg
        SEG = cfg.SEG
        for (g0, g1) in ((0, 2 * SEG), (2 * SEG, 3 * SEG)):
            self.mla_group(li, g0, g1)
        self.mla_m3(li)

    def mla_group(self, li, g0, g1):
        T, cfg = self.T, self.cfg
        A = self.arena
        A.reset()
        SEG = cfg.SEG
        TT = 512
        TG = g1 - g0
        gname = f"nmix{li}"
        cqT = A.alloc([128, 3, TG], BF16, "cqT")
        ckvT = A.alloc([128, 2, TG], BF16, "ckvT")
        KR = A.alloc([128, TG], BF16, "KR")
        mark = A.off
        wdq = A.alloc([128, KC, 384], BF16, "wdq")
        wdkv = A.alloc([128, KC, 256], BF16, "wdkv")
        wkr0 = A.alloc([128, KC, 96], BF16, "wkr0")
        wkr1 = A.alloc([128, KC, 96], BF16, "wkr1")
        wdqb = [Buf() for _ in range(KC)]
        wdkvb = [Buf() for _ in range(KC)]
        wkr0b = [Buf() for _ in range(KC)]
        wkr1b = [Buf() for _ in range(KC)]
        self.load_w(wdq, wdqb, self.mla_wdq, KC)
        self.load_w(wdkv, wdkvb, self.mla_wdkv, KC)
        self.load_w(wkr0, wkr0b, self.mla_wkr[0], KC)
        self.load_w(wkr1, wkr1b, self.mla_wkr[1], KC)
        xr = Ring(A, 2, [128, KC, TT], F32, "x")
        hr = Ring(A, 1, [128, KC, TT], BF16, "h")
        sqr = Ring(A, 1, [128, KC, TT], BF16, "sq")
        rs = Ring(A, 2, [128, TT], F32, "rs")
        cqfr = Ring(A, 1, [128, 3, TT], F32, "cqf")
        ckvfr = Ring(A, 1, [128, 2, TT], F32, "ckvf")
        ropr = Ring(A, 2, [128, 2, TT], F32, "rope")
        t12r = Ring(A, 2, [128, 2, TT], F32, "t12")
        resb = Buf("mla_res")
        for it in range(TG // TT):
            tl = it * TT
            t0 = g0 + tl
            xt, xb = xr.next()
            T.add("sp", lambda e, xt=xt, t0=t0, src=self.xT: e.dma_start(out=xt[:, :, :], in_=src[:, :, t0:t0 + TT].rearrange("c p t -> p c t")),
                  r=self.xblocks(self.xbuf, t0, TT), w=[xb], dma=True)
            rp, rpb = ropr.next()
            for i in range(2):
                T.add("sp", lambda e, rp=rp, i=i, t0=t0: e.dma_start(out=rp[64:96, i, :], in_=self.rope_d[i][:, t0:t0 + TT]), w=[rpb], dma=True)
            h, hb = hr.next()
            sq, sqb = sqr.next()
            rstd, rb = rs.next()
            self.rmsnorm(xt, xb, KC, TT, gname, h, hb, sq, sqb, rstd, rb, D)
            cqf, cqfb = cqfr.next()
            ckvf, ckvfb = ckvfr.next()
            for (w_, wb_, n, dstf, dstb) in ((wdq, wdqb, 3, cqf, cqfb), (wdkv, wdkvb, 2, ckvf, ckvfb)):
                for oc in range(n):
                    bk = self.bank()

                    def mm(e, w_=w_, oc=oc, bk=bk, h=h):
                        for k in range(KC):
                            ins = e.matmul(self.psum[:, bk, :], lhsT=w_[:, k, oc * 128:(oc + 1) * 128], rhs=h[:, k, :], start=(k == 0), stop=(k == KC - 1))
                        return ins
                    T.add("pe", mm, r=[hb] + wb_, w=[self.pb[bk]])
                    T.add("act", lambda e, dstf=dstf, oc=oc, bk=bk: e.activation(out=dstf[:, oc, :], in_=self.psum[:, bk, :], func=AF.Copy),
                          r=[self.pb[bk]], w=[dstb])
            sq2, sq2b = sqr.next()
            rstd2, rb2 = rs.next()
            self.rmsnorm(cqf, cqfb, 3, TT, "gq", cqT[:, :, tl:tl + TT], resb, sq2, sq2b, rstd2, rb2, 384)
            sq3, sq3b = sqr.next()
            rstd3, rb3 = rs.next()
            self.rmsnorm(ckvf, ckvfb, 2, TT, "gkv", ckvT[:, :, tl:tl + TT], resb, sq3, sq3b, rstd3, rb3, 256)
            ba_, bb_ = self.bank(), self.bank()

            def mkr(e, ba_=ba_, bb_=bb_, h=h):
                for (w_, bk) in ((wkr0, ba_), (wkr1, bb_)):
                    for k in range(KC):
                        ins = e.matmul(self.psum[0:96, bk, :], lhsT=w_[:, k, :], rhs=h[:, k, :], start=(k == 0), stop=(k == KC - 1))
                return ins
            T.add("pe", mkr, r=[hb] + wkr0b + wkr1b, w=[self.pb[ba_], self.pb[bb_]])
            t12, t12b = t12r.next()
            T.add("dve", lambda e, t12=t12, ba_=ba_, rp=rp: e.tensor_tensor(out=t12[64:96, 0, :], in0=self.psum[64:96, ba_, :], in1=rp[64:96, 0, :], op=ALU.mult),
                  r=[self.pb[ba_], rpb], w=[t12b])
            T.add("dve", lambda e, t12=t12, bb_=bb_, rp=rp: e.tensor_tensor(out=t12[64:96, 1, :], in0=self.psum[64:96, bb_, :], in1=rp[64:96, 1, :], op=ALU.mult),
                  r=[self.pb[bb_], rpb], w=[t12b])
            T.add("dve", lambda e, t12=t12, tl=tl: e.tensor_tensor(out=KR[64:96, tl:tl + TT], in0=t12[64:96, 0, :], in1=t12[64:96, 1, :], op=ALU.add),
                  r=[t12b], w=[resb])
        T.barrier()
        A.off = mark
        wuq = A.alloc([128, 3, 1536], BF16, "wuq")
        wuqs = A.alloc([128, 3, 1536], BF16, "wuqs")
        wuk = A.alloc([128, 2, 1024], BF16, "wuk")
        wuv = A.alloc([128, 2, 1024], BF16, "wuv")
        wuqb = [Buf() for _ in range(3)]
        wuqsb = [Buf() for _ in range(3)]
        wukb = [Buf() for _ in range(2)]
        wuvb = [Buf() for _ in range(2)]
        self.load_w(wuk, wukb, self.mla_wuk, 2)
        self.load_w(wuv, wuvb, self.mla_wuv, 2)
        self.load_w(wuq, wuqb, self.mla_wuq[0], 3)
        self.load_w(wuqs, wuqsb, self.mla_wuq[1], 3)
        NKC = TG // 128
        Kg = A.alloc([128, 4, TG], BF16, "Kg")
        Vg = A.alloc([128, NKC, 4, 128], BF16, "Vg")
        kgm = [Buf(f"kgm{i}") for i in range(4)]
        kgr = [Buf(f"kgr{i}") for i in range(4)]
        vgs = [Buf("vg0"), Buf("vg1")]
        qhr = Ring(A, 2, [128, TT], BF16, "qh")
        ropr = Ring(A, 2, [128, 2, TT], F32, "rope")
        t12r = Ring(A, 2, [128, 2, TT], F32, "t12")
        ptr = Ring(A, 4, [128, TT], BF16, "pt")
        rdr = Ring(A, 2, [128, TT], F32, "rdn")
        bcr = Ring(A, 2, [128, TT], F32, "bcs")
        onr = Ring(A, 3, [128, TT], BF16, "on")
        T.add("dve", lambda e: e.memset(Vg[:, :, :, 64:128], 1.0), w=vgs)
        T.add("pool", lambda e: e.memset(Kg[64:128, :, :], 0.0), w=kgr)
        for (qh_, qhb_) in zip(qhr.tiles, qhr.bufs):
            T.add("pool", lambda e, qh_=qh_: e.memset(qh_[64:128, :], 0.0), w=[qhb_])
        scale = float(96 ** -0.5)
        st_ = {"sbank": 0, "obank": 0}
        for g in range(4):
            for it in range(TG // TT):
                tl = it * TT
                for hh in range(4):
                    hd = 4 * g + hh
                    bk = self.bank()

                    def mk(e, hd=hd, bk=bk, tl=tl):
                        for k in range(2):
                            ins = e.matmul(self.psum[0:64, bk, :], lhsT=wuk[:, k, hd * 64:(hd + 1) * 64], rhs=ckvT[:, k, tl:tl + TT], start=(k == 0), stop=(k == 1))
                        return ins
                    T.add("pe", mk, r=wukb, w=[self.pb[bk]])
                    if False:
                        pass
                    else:
                        T.add("dve", lambda e, hh=hh, bk=bk, tl=tl: e.tensor_copy(out=Kg[0:64, hh, tl:tl + TT], in_=self.psum[0:64, bk, :]),
                              r=[self.pb[bk]], w=[kgm[hh]])
                    T.add("pool", lambda e, hh=hh, tl=tl: e.tensor_copy(out=Kg[64:96, hh, tl:tl + TT], in_=KR[64:96, tl:tl + TT]), w=[kgr[hh]])
                for st in range(TT // 128):
                    kc = (tl // 128) + st
                    bk = self.bank()

                    def mv(e, kc=kc, bk=bk, g=g):
                        for k in range(2):
                            ins = e.matmul(self.psum[:, bk, 0:256], lhsT=ckvT[:, k, kc * 128:(kc + 1) * 128], rhs=wuv[:, k, g * 256:(g + 1) * 256],
                                           start=(k == 0), stop=(k == 1))
                        return ins
                    T.add("pe", mv, r=wuvb, w=[self.pb[bk]])
                    src = self.psum[:, bk, 0:256].rearrange("p (a b) -> p a b", a=4)
                    if False:
                        pass
                    else:
                        T.add("dve", lambda e, kc=kc, src=src: e.tensor_copy(out=Vg[:, kc, :, 0:64], in_=src), r=[self.pb[bk]], w=[vgs[1]])
            units = [(it, hh) for it in range(TG // TT) for hh in range(4)]
            rope_cache = {}

            def prep(u, g=g):
                it, hh = units[u]
                tl = it * TT
                t0 = g0 + tl
                hd = 4 * g + hh
                if it not in rope_cache:
                    rp, rpb = ropr.next()
                    for i in range(2):
                        T.add("sp", lambda e, rp=rp, i=i, t0=t0: e.dma_start(out=rp[64:96, i, :], in_=self.rope_d[i][:, t0:t0 + TT]), w=[rpb], dma=True)
                    rope_cache.clear()
                    rope_cache[it] = (rp, rpb)
                rp, rpb = rope_cache[it]

                def mq(e, hd=hd, tl=tl):
                    for (w_, bk) in ((wuq, 5), (wuqs, 6)):
                        for k in range(3):
                            ins = e.matmul(self.psum[0:96, bk, :], lhsT=w_[:, k, hd * 96:(hd + 1) * 96], rhs=cqT[:, k, tl:tl + TT], start=(k == 0), stop=(k == 2))
                    return ins
                T.add("pe", mq, r=wuqb + wuqsb, w=[self.pb[5], self.pb[6]])
                qh, qhb = qhr.next()
                t12, t12b = t12r.next()
                T.add("dve", lambda e, qh=qh: e.tensor_copy(out=qh[0:64, :], in_=self.psum[0:64, 5, :]), r=[self.pb[5]], w=[qhb])
                T.add("dve", lambda e, t12=t12, rp=rp: e.tensor_tensor(out=t12[64:96, 0, :], in0=self.psum[64:96, 5, :], in1=rp[64:96, 0, :], op=ALU.mult),
                      r=[self.pb[5], rpb], w=[t12b])
                T.add("dve", lambda e, t12=t12, rp=rp: e.tensor_tensor(out=t12[64:96, 1, :], in0=self.psum[64:96, 6, :], in1=rp[64:96, 1, :], op=ALU.mult),
                      r=[self.pb[6], rpb], w=[t12b])
                T.add("dve", lambda e, t12=t12, qh=qh: e.tensor_tensor(out=qh[64:96, :], in0=t12[64:96, 0, :], in1=t12[64:96, 1, :], op=ALU.add),
                      r=[t12b], w=[qhb])
                return qh, qhb

            def finish_head(bo, hd, t0):
                rdn, rdb = rdr.next()
                bcs, bcb = bcr.next()
                on, onb = onr.next()
                T.add("dve", lambda e, rdn=rdn, bo=bo: e.reciprocal(out=rdn[64:65, :], in_=self.psum[64:65, bo, :]), r=[self.pb[bo]], w=[rdb])

                def rest():
                    T.add("pe", lambda e, rdn=rdn: e.matmul(self.psum[0:64, 7, :], lhsT=self.onesf[64:65, 0:64], rhs=rdn[64:65, :], start=True, stop=True),
                          r=[rdb], w=[self.pb[7]])
                    T.add("dve", lambda e, bcs=bcs: e.tensor_copy(out=bcs[0:64, :], in_=self.psum[0:64, 7, :]), r=[self.pb[7]], w=[bcb])
                    T.add("dve", lambda e, on=on, bcs=bcs, bo=bo: e.tensor_tensor(out=on[0:64, :], in0=self.psum[0:64, bo, :], in1=bcs[0:64, :], op=ALU.mult),
                          r=[self.pb[bo], bcb], w=[onb])
                    T.add("pool", lambda e, on=on, hd=hd, t0=t0: e.dma_start(out=self.oT_s[hd // 2, (hd % 2) * 64:(hd % 2) * 64 + 64, t0:t0 + TT], in_=on[0:64, :]),
                          r=[onb], dma=True)
                return rest

            nxt = prep(0)
            deferred = None
            for u, (it, hh) in enumerate(units):
                tl = it * TT
                t0 = g0 + tl
                hd = 4 * g + hh
                qh, qhb = nxt
                bo = 3 + (st_["obank"] % 2)
                st_["obank"] += 1
                pend = []
                for kc in range(NKC):
                    bs = st_["sbank"] % 3
                    st_["sbank"] += 1
                    T.add("pe", lambda e, bs=bs, hh=hh, kc=kc, qh=qh: e.matmul(self.psum[:, bs, :], lhsT=Kg[:, hh, kc * 128:(kc + 1) * 128], rhs=qh[:, :],
                                                                               start=True, stop=True), r=[qhb, kgm[hh], kgr[hh]], w=[self.pb[bs]])
                    if len(pend) >= 2:
                        pend.pop(0)()
                    if kc == 2 and deferred is not None:
                        deferred()
                        deferred = None
                    if kc == NKC // 2 and u + 1 < len(units):
                        nxt = prep(u + 1)
                    pt, ptb = ptr.next()
                    cross = (g0 == 0) and ((kc * 128) // SEG != tl // SEG)
                    bname = "xbias" if cross else "zero"
                    T.add("act", lambda e, pt=pt, bs=bs, bname=bname: e.activation(out=pt[:, :], in_=self.psum[:, bs, :], func=AF.Exp, scale=scale,
                                                                                bias=self.pcol(bname)), r=[self.pb[bs]], w=[ptb])

                    def pv(pt=pt, ptb=ptb, kc=kc, hh=hh, bo=bo):
                        T.add("pe", lambda e: e.matmul(self.psum[:, bo, :], lhsT=Vg[:, kc, hh, :], rhs=pt[:, :], start=(kc == 0), stop=(kc == NKC - 1)),
                              r=[ptb] + vgs, w=[self.pb[bo]])
                    pend.append(pv)
                for p_ in pend:
                    p_()
                if deferred is not None:
                    deferred()
                deferred = finish_head(bo, hd, t0)
            if deferred is not None:
                deferred()
        T.barrier()

    def mla_m3(self, li):
        T, cfg = self.T, self.cfg
        A = self.arena
        A.reset()
        TT = 512
        wo = A.alloc([128, KC, D], BF16, "wo")
        wob = [Buf() for _ in range(KC)]
        self.load_w(wo, wob, self.mla_wo, KC)
        xr = Ring(A, 2, [128, KC, TT], F32, "x")
        orr = Ring(A, 2, [128, KC, TT], BF16, "o")
        for it in range(cfg.NT // TT):
            t0 = it * TT
            xt, xb = xr.next()
            ot, ob = orr.next()
            T.add("sp", lambda e, xt=xt, t0=t0, src=self.xT: e.dma_start(out=xt[:, :, :], in_=src[:, :, t0:t0 + TT].rearrange("c p t -> p c t")),
                  r=self.xblocks(self.xbuf, t0, TT), w=[xb], dma=True)
            T.add("sp", lambda e, ot=ot, t0=t0: e.dma_start(out=ot[:, :, :], in_=self.oT_s[:, :, t0:t0 + TT].rearrange("c p t -> p c t")), w=[ob], dma=True)
            for oc in range(KC):
                bk = self.bank()

                def my(e, oc=oc, bk=bk, ot=ot):
                    for k in range(KC):
                        ins = e.matmul(self.psum[:, bk, :], lhsT=wo[:, k, oc * 128:(oc + 1) * 128], rhs=ot[:, k, :], start=(k == 0), stop=(k == KC - 1))
                    return ins
                T.add("pe", my, r=[ob] + wob, w=[self.pb[bk]])
                T.add("dve", lambda e, oc=oc, bk=bk, xt=xt: e.tensor_tensor(out=xt[:, oc, :], in0=xt[:, oc, :], in1=self.psum[:, bk, :], op=ALU.add),
                      r=[self.pb[bk], xb], w=[xb])
            T.add("pool", lambda e, xt=xt, t0=t0, dst=self.xT: e.dma_start(out=dst[:, :, t0:t0 + TT].rearrange("c p t -> p c t"), in_=xt[:, :, :]),
                  r=[xb], w=self.xblocks(self.xbuf, t0, TT), dma=True)
        T.barrier()

    def build(self):
        cfg = self.cfg
        self.setup()
        self.phase_in()
        for li in cfg.layers:
            m = li % 3
            if li not in getattr(cfg, "mixers", cfg.layers):
                pass
            elif m == 0 and hasattr(self, "phase_na"):
                self.phase_na(li)
            elif m == 1 and hasattr(self, "phase_lru"):
                self.phase_lru(li)
            elif m == 2 and hasattr(self, "phase_mla"):
                self.phase_mla(li)
            if cfg.do_ffn:
                self.phase_ffn(li)
        self.phase_out()
        nc, T = self.nc, self.T
        from contextlib import ExitStack
        with ExitStack() as es:
            sc = {c: es.enter_context(nc.semaphore(f"s_{c}")) for c in COMPUTE}
            sd = {q: [es.enter_context(nc.semaphore(f"d_{q}{i}")) for i in range(T.n_dma_sems)] for q in ("sp", "pool")}
            T.finalize(sc, sd)
            block = es.enter_context(nc.Block())

            @block.sync
            def _(e):
                T.emit_queue("sp", e)

            @block.tensor
            def _(e):
                T.emit_queue("pe", e)

            @block.scalar
            def _(e):
                T.emit_queue("act", e)

            @block.vector
            def _(e):
                T.emit_queue("dve", e)

            @block.gpsimd
            def _(e):
                T.emit_queue("pool", e)
        return nc


def core_segments(cfg, c):
    nA = cfg.n_cores // 2
    if c < nA:
        return [("p", c, 0), ("p", c, 1), ("s", c, 0)]
    b = c - nA
    return [("s", nA + 3 * b + i, 0) for i in range(3)]


def col_layout(v, ncol):
    return np.ascontiguousarray(np.asarray(v, np.float32).reshape(ncol, 128).T)


def build_pvec(cfg, pv, inp, is_a):
    P = np.zeros((128, pv.n), np.float32)

    def put(name, arr):
        o = pv.off[name]
        P[:, o:o + arr.shape[1]] = arr
    for i in range(4):
        put(f"nmix{i}", col_layout(inp["norm_mix"][i], KC))
        put(f"nffn{i}", col_layout(inp["norm_ffn"][i], KC))
    put("nfin", col_layout(inp["norm_final"], KC))
    cw = np.asarray(inp["lru_conv_w"][0], np.float32)
    put("convw", np.ascontiguousarray(cw.reshape(4, LC, 128).transpose(2, 1, 0).reshape(128, LC * 4)))
    put("convb", col_layout(inp["lru_conv_b"][0], LC))
    for d in range(2):
        put(f"ba{d}", col_layout(inp["lru_b_a"][0, d], LC))
        put(f"bx{d}", col_layout(inp["lru_b_x"][0, d], LC))
        put(f"lam{d}", col_layout(inp["lru_lam"][0, d], LC))
    put("gq", col_layout(inp["mla_g_q"][0], 3))
    put("gkv", col_layout(inp["mla_g_kv"][0], 2))
    put("flag", np.full((128, 1), 1.0 if is_a else 0.0, np.float32))
    put("xbias", np.full((128, 1), 0.0 if is_a else NEG, np.float32))
    put("nab", na_boundary_bias(cfg, is_a))
    put("eps", np.full((128, 1), EPS, np.float32))
    put("one", np.full((128, 1), 1.0, np.float32))
    return P


def na_boundary_bias(cfg, is_a):
    R = cfg.R
    out = np.zeros((128, 4, 6, 2), np.float32)
    pairs = [R - 4, R - 2, R, R + 2]
    starts = [R - 8, R - 8, R - 4, R - 4]
    for pi, (r, k0) in enumerate(zip(pairs, starts)):
        for c in range(6):
            for j in range(2):
                krow = k0 + 2 * c + j
                for jp in range(2):
                    q = r + jp
                    if is_a:
                        rows = 2 * R
                        rs = min(max(q - 4, 0), rows - 8)
                        ok = rs <= krow < rs + 8
                    else:
                        base = 0 if q < R else R
                        rs = min(max(q - base - 4, 0), R - 8) + base
                        ok = rs <= krow < rs + 8
                    out[j * 64:(j + 1) * 64, pi, c, jp] = 0.0 if ok else NEG
    return out.reshape(128, 48)


def na_bias_table(rpb):
    rpb = np.asarray(rpb, np.float32)
    qc = np.arange(GW)
    ws = np.clip(qc - 8, 0, GW - 16)
    kc = np.arange(GW)
    valid = (kc[None, :] >= ws[:, None]) & (kc[None, :] < ws[:, None] + 16)
    dc = np.clip(kc[None, :] - qc[:, None] + 15, 0, 30)
    out = np.full((NH, 2, GW, 16, GW), NEG, np.float32)
    for jp in range(2):
        for e in range(16):
            dr = e - jp
            if 0 <= dr <= 14:
                vals = rpb[:, dr][:, dc]
                out[:, jp, :, e, :] = np.where(valid[None], vals, NEG)
    tc0 = out[:, :, :, 3:5, :].copy()
    tc0[:, 1, :, 0, :] = NEG
    tc4 = out[:, :, :, 11:13, :].copy()
    tc4[:, 0, :, :, :] = NEG
    tc4[:, 1, :, 1, :] = NEG
    blocks = []
    for bi in range(7):
        b = 2 * bi + 1
        blk = out[:, :, :, b:b + 2, :]
        blocks.append(blk.transpose(0, 3, 4, 1, 2).reshape(NH, 128, 128))
    inter = [tc0] + [out[:, :, :, b:b + 2, :] for b in (5, 7, 9)] + [tc4]
    for blk in inter:
        blocks.append(blk.transpose(0, 3, 4, 1, 2).reshape(NH, 128, 128))
    return np.ascontiguousarray(np.concatenate(blocks, axis=2))


def lru_band(w):
    w = np.asarray(w, np.float32)
    dense = np.zeros((LRU_C, LRU_C), np.float32)
    for n in range(16):
        dense[n * 88:(n + 1) * 88, n * 88:(n + 1) * 88] = w[n]
    out = np.zeros((128, LC, 3, 128), np.float32)
    for m in range(LC):
        for kk in range(3):
            k = m + kk - 1
            if 0 <= k < LC:
                out[:, m, kk, :] = dense[k * 128:(k + 1) * 128, m * 128:(m + 1) * 128]
    return out.reshape(128, LC * 3 * 128)


def rope_tables(cfg, is_a):
    SEG = cfg.SEG
    pos = np.concatenate([np.arange(SEG), np.arange(SEG) + (SEG if is_a else 0), np.arange(SEG)]).astype(np.float32)
    inv = (10000.0 ** (-np.arange(0, 32, 2, dtype=np.float32) / 32)).astype(np.float32)
    ang = pos[None, :] * inv[:, None]
    c, s = np.cos(ang).astype(np.float32), np.sin(ang).astype(np.float32)
    C = np.concatenate([c, c], 0)
    S = np.concatenate([-s, s], 0)
    return np.ascontiguousarray(np.stack([C, S], 0))


_SHARED_CACHE = {}


def prepare_inputs(cfg, inp):
    pv = pv_layout()
    inp = {k: np.asarray(v) for k, v in inp.items()}
    f32 = lambda a: np.ascontiguousarray(a, dtype=np.float32)
    sh = {}
    sh["ident"] = np.eye(128, dtype=np.float32)
    sh["ffn_wg"] = f32(inp["ffn_w_gate"])
    sh["ffn_wu"] = f32(inp["ffn_w_up"])
    sh["ffn_wd"] = f32(inp["ffn_w_down"])
    sh["na_wqkv"] = f32(inp["na_w_qkv"])
    sh["na_wo"] = f32(inp["na_w_o"])
    sh["na_btab"] = np.stack([na_bias_table(inp["na_rpb"][j]) for j in range(2)], 0)
    sh["lru_win"] = f32(inp["lru_w_in"][0])
    sh["lru_band"] = np.stack([lru_band(inp["lru_w_a"][0, 0]), lru_band(inp["lru_w_x"][0, 0]),
                               lru_band(inp["lru_w_a"][0, 1]), lru_band(inp["lru_w_x"][0, 1])], 0)
    sh["lru_wout"] = f32(inp["lru_w_out"][0])
    sh["mla_wdq"] = f32(inp["mla_w_dq"][0])
    wdkv = f32(inp["mla_w_dkv"][0])
    sh["mla_wdkv"] = f32(wdkv[:, :256])
    wkr = np.zeros((2, D, 96), np.float32)
    wkr[0, :, 64:96] = wdkv[:, 256:288]
    wkr[1, :, 64:80] = wdkv[:, 272:288]
    wkr[1, :, 80:96] = wdkv[:, 256:272]
    sh["mla_wkr"] = wkr
    wuq = f32(inp["mla_w_uq"][0]).reshape(384, NH, 96)
    wuq_sw = wuq.copy()
    wuq_sw[:, :, 64:80] = wuq[:, :, 80:96]
    wuq_sw[:, :, 80:96] = wuq[:, :, 64:80]
    sh["mla_wuq"] = np.ascontiguousarray(np.stack([wuq.reshape(384, 1536), wuq_sw.reshape(384, 1536)], 0))
    wukv = f32(inp["mla_w_ukv"][0]).reshape(256, NH, 128)
    sh["mla_wuk"] = np.ascontiguousarray(wukv[:, :, :64].reshape(256, 1024))
    sh["mla_wuv"] = np.ascontiguousarray(wukv[:, :, 64:].reshape(256, 1024))
    sh["mla_wo"] = f32(inp["mla_w_o"][0])
    xp, xs = inp["x_prompt"], inp["x_sample"]
    SEG = cfg.SEG
    in_maps = []
    for c in range(cfg.n_cores):
        segs = core_segments(cfg, c)
        is_a = segs[0][0] == "p"
        parts = []
        for (g, i, hlf) in segs:
            if g == "p":
                parts.append(xp[i, hlf * SEG:(hlf + 1) * SEG])
            else:
                parts.append(xs[i])
        m = dict(sh)
        m["x_tok"] = np.ascontiguousarray(np.concatenate(parts, 0), dtype=np.float32)
        m["pvec"] = build_pvec(cfg, pv, inp, is_a)
        m["rope"] = rope_tables(cfg, is_a)
        in_maps.append(m)
    return in_maps


def gather_outputs(cfg, results, n_prompt, n_sample):
    SEG = cfg.SEG
    yp = np.zeros((n_prompt, 2 * SEG, D), np.float32)
    ys = np.zeros((n_sample, SEG, D), np.float32)
    for c in range(cfg.n_cores):
        y = np.asarray(results[c]["y_tok"], np.float32)
        for si, (g, i, hlf) in enumerate(core_segments(cfg, c)):
            blk = y[si * SEG:(si + 1) * SEG]
            if g == "p":
                yp[i, hlf * SEG:(hlf + 1) * SEG] = blk
            else:
                ys[i] = blk
    return yp, ys


_PROG_CACHE = {}


def kernel(**inputs):
    cfg = Cfg()
    in_maps = prepare_inputs(cfg, inputs)
    nc = Prog(cfg).build()
    res = run_bass_kernel_spmd(nc, in_maps, core_ids=list(range(cfg.n_cores)))
    yp, ys = gather_outputs(cfg, res.results, inputs["x_prompt"].shape[0], inputs["x_sample"].shape[0])
    return (yp, ys)
```
